# Optimizing a Trainium2 kernel written in Bass

```python
import jax, jax.numpy as jnp
from jax import lax
import numpy as np

D_MODEL = 2048
BATCH = 2
SEQ = 8192
DEPTH = 1

CHUNK = 64
Q_BLOCK = 128
EPS = 1e-6

CONV_WIDTH = 1024
CONV_K = 3

MLA_HEADS = 16
Q_LORA = 512
KV_LORA = 512
QK_NOPE = 128
QK_ROPE = 64
V_HEAD = 128
MLA_WIDTH = MLA_HEADS * V_HEAD
ROPE_THETA = 10000.0

MEM_TOKENS = 256
MEM_HEADS = 4
MEM_HEAD_DIM = 256
MEM_WIDTH = MEM_HEADS * MEM_HEAD_DIM

IN_SPLITS = (
    CONV_WIDTH, CONV_WIDTH, CONV_WIDTH, CONV_WIDTH,
    Q_LORA, KV_LORA, QK_ROPE, MLA_WIDTH,
    MEM_WIDTH, MEM_WIDTH,
    D_MODEL, D_MODEL, D_MODEL,
)
IN_WIDTH = sum(IN_SPLITS)

kernel_name = "hybrid_conv_mla_memory_block"


def rms_norm(x, g):
    xf = x.astype(jnp.float32)
    y = xf * lax.rsqrt(jnp.mean(xf * xf, axis=-1, keepdims=True) + EPS)
    return (y * g.astype(jnp.float32)).astype(x.dtype)


def apply_rope(x, cos, sin):
    x1, x2 = jnp.split(x.astype(jnp.float32), 2, axis=-1)
    return jnp.concatenate([x1 * cos - x2 * sin, x2 * cos + x1 * sin], axis=-1).astype(x.dtype)


def causal_depthwise_conv(u, w):
    k, c = w.shape
    return lax.conv_general_dilated(
        u, w[:, None, :].astype(u.dtype), window_strides=(1,), padding=[(k - 1, 0)],
        dimension_numbers=("NWC", "WIO", "NWC"), feature_group_count=c)


def mla_attention(c_q, c_kv, k_rope_raw, cos, sin, q_norm_g, w_uq, kv_norm_g, w_ukv,
                  qn_nope_g, qn_rope_g, kn_nope_g, kn_rope_g):
    b, s, _ = c_q.shape
    q = (rms_norm(c_q, q_norm_g) @ w_uq).reshape(b, s, MLA_HEADS, QK_NOPE + QK_ROPE)
    q_nope = rms_norm(q[..., :QK_NOPE], qn_nope_g)
    q_rope = apply_rope(rms_norm(q[..., QK_NOPE:], qn_rope_g), cos[:, :, None], sin[:, :, None])
    kv = (rms_norm(c_kv, kv_norm_g) @ w_ukv).reshape(b, s, MLA_HEADS, QK_NOPE + V_HEAD)
    k_nope = rms_norm(kv[..., :QK_NOPE], kn_nope_g)
    v = kv[..., QK_NOPE:]
    k_rope = apply_rope(rms_norm(k_rope_raw, kn_rope_g), cos, sin)
    scale = (QK_NOPE + QK_ROPE) ** -0.5
    n_blk = s // Q_BLOCK
    qn_blocks = q_nope.reshape(b, n_blk, Q_BLOCK, MLA_HEADS, QK_NOPE).transpose(1, 0, 2, 3, 4)
    qr_blocks = q_rope.reshape(b, n_blk, Q_BLOCK, MLA_HEADS, QK_ROPE).transpose(1, 0, 2, 3, 4)
    k_chunk = jnp.arange(s) // CHUNK

    def attend(args):
        qn, qr, blk = args
        sc = (jnp.einsum("bqhd,bkhd->bhqk", qn, k_nope, preferred_element_type=jnp.float32)
              + jnp.einsum("bqhr,bkr->bhqk", qr, k_rope, preferred_element_type=jnp.float32))
        q_chunk = (blk * Q_BLOCK + jnp.arange(Q_BLOCK)) // CHUNK
        allowed = k_chunk[None, :] <= q_chunk[:, None]
        p = jax.nn.softmax(jnp.where(allowed, sc * scale, -jnp.inf), axis=-1)
        return jnp.einsum("bhqk,bkhd->bqhd", p.astype(v.dtype), v)

    o = lax.map(attend, (qn_blocks, qr_blocks, jnp.arange(n_blk)))
    return o.transpose(1, 0, 2, 3, 4).reshape(b, s, MLA_WIDTH)


def memory_attention(q_raw, mem, mem_norm_g, w_mem_kv, qn_g, kn_g):
    b, s, _ = q_raw.shape
    m = mem.shape[1]
    q = rms_norm(q_raw.reshape(b, s, MEM_HEADS, MEM_HEAD_DIM), qn_g)
    k, v = jnp.split(rms_norm(mem, mem_norm_g) @ w_mem_kv, 2, axis=-1)
    k = rms_norm(k.reshape(b, m, MEM_HEADS, MEM_HEAD_DIM), kn_g)
    v = v.reshape(b, m, MEM_HEADS, MEM_HEAD_DIM)
    sc = jnp.einsum("bqhd,bmhd->bhqm", q, k, preferred_element_type=jnp.float32) * (MEM_HEAD_DIM ** -0.5)
    p = jax.nn.softmax(sc, axis=-1)
    return jnp.einsum("bhqm,bmhd->bqhd", p.astype(v.dtype), v).reshape(b, s, MEM_WIDTH)


def setup_inputs(seed: int = 0) -> dict:
    key = jax.random.key(seed)
    ks = jax.random.split(key, 24)
    f32 = jnp.float32

    def w(k, shape, fan_in):
        return jax.random.normal(k, shape, f32) * (fan_in ** -0.5)

    def gain(k, shape):
        return 1.0 + 0.02 * jax.random.normal(k, shape, f32)

    x = jax.random.normal(ks[0], (BATCH, SEQ, D_MODEL), f32)
    offsets = jax.random.randint(ks[1], (BATCH, 1), 0, 64, dtype=jnp.int32) * CHUNK
    positions = (offsets + jnp.arange(SEQ, dtype=jnp.int32)[None, :]).astype(jnp.int32)
    mem = jax.random.normal(ks[2], (BATCH, MEM_TOKENS, D_MODEL), f32)
    L = DEPTH
    return {
        "x": x,
        "positions": positions,
        "mem": mem,
        "norm_g": gain(ks[3], (L, D_MODEL)),
        "w_in": w(ks[4], (L, D_MODEL, IN_WIDTH), D_MODEL),
        "conv_w": w(ks[5], (L, CONV_K, CONV_WIDTH), CONV_K),
        "w_conv_out": w(ks[6], (L, CONV_WIDTH, D_MODEL), CONV_WIDTH),
        "mla_q_norm_g": gain(ks[7], (L, Q_LORA)),
        "w_uq": w(ks[8], (L, Q_LORA, MLA_HEADS * (QK_NOPE + QK_ROPE)), Q_LORA),
        "mla_kv_norm_g": gain(ks[9], (L, KV_LORA)),
        "w_ukv": w(ks[10], (L, KV_LORA, MLA_HEADS * (QK_NOPE + V_HEAD)), KV_LORA),
        "mla_qn_nope_g": gain(ks[11], (L, QK_NOPE)),
        "mla_qn_rope_g": gain(ks[12], (L, QK_ROPE)),
        "mla_kn_nope_g": gain(ks[13], (L, QK_NOPE)),
        "mla_kn_rope_g": gain(ks[14], (L, QK_ROPE)),
        "w_mla_out": w(ks[15], (L, MLA_WIDTH, D_MODEL), MLA_WIDTH),
        "mem_norm_g": gain(ks[16], (L, D_MODEL)),
        "w_mem_kv": w(ks[17], (L, D_MODEL, 2 * MEM_WIDTH), D_MODEL),
        "mem_qn_g": gain(ks[18], (L, MEM_HEAD_DIM)),
        "mem_kn_g": gain(ks[19], (L, MEM_HEAD_DIM)),
        "w_mem_out": w(ks[20], (L, MEM_WIDTH, D_MODEL), MEM_WIDTH),
        "w_o": w(ks[21], (L, D_MODEL, D_MODEL), D_MODEL),
    }


def reference(x, positions, mem, norm_g, w_in, conv_w, w_conv_out, mla_q_norm_g, w_uq,
              mla_kv_norm_g, w_ukv, mla_qn_nope_g, mla_qn_rope_g, mla_kn_nope_g, mla_kn_rope_g,
              w_mla_out, mem_norm_g, w_mem_kv, mem_qn_g, mem_kn_g, w_mem_out, w_o):
    half = QK_ROPE // 2
    inv_freq = jnp.power(ROPE_THETA, -jnp.arange(half, dtype=jnp.float32) / half)
    ang = positions.astype(jnp.float32)[..., None] * inv_freq
    cos, sin = jnp.cos(ang), jnp.sin(ang)
    split_at = np.cumsum(IN_SPLITS)[:-1].tolist()

    for l in range(DEPTH):
        h = rms_norm(x, norm_g[l])
        proj = h @ w_in[l]
        (c_gate, b_gate, u, conv_z, c_q, c_kv, k_rope_raw, mla_z,
         mem_q, mem_z, g_conv, g_mla, g_mem) = jnp.split(proj, split_at, axis=-1)

        conv_y = b_gate * causal_depthwise_conv(c_gate * u, conv_w[l])
        o_conv = (conv_y * jax.nn.silu(conv_z)) @ w_conv_out[l]

        mla_y = mla_attention(c_q, c_kv, k_rope_raw, cos, sin, mla_q_norm_g[l], w_uq[l],
                              mla_kv_norm_g[l], w_ukv[l], mla_qn_nope_g[l], mla_qn_rope_g[l],
                              mla_kn_nope_g[l], mla_kn_rope_g[l])
        o_mla = (mla_y * jax.nn.silu(mla_z)) @ w_mla_out[l]

        mem_y = memory_attention(mem_q, mem, mem_norm_g[l], w_mem_kv[l], mem_qn_g[l], mem_kn_g[l])
        o_mem = (mem_y * jax.nn.silu(mem_z)) @ w_mem_out[l]

        merged = (jax.nn.sigmoid(g_conv) * o_conv + jax.nn.sigmoid(g_mla) * o_mla
                  + jax.nn.sigmoid(g_mem) * o_mem)
        x = x + merged @ w_o[l]
    return x
```

```python
import numpy as np
from contextlib import ExitStack
import concourse.bass as bass
import concourse.mybir as mybir
from concourse.bass_utils import run_bass_kernel_spmd

F32 = mybir.dt.float32
BF16 = mybir.dt.bfloat16
I32 = mybir.dt.int32
AF = mybir.ActivationFunctionType
ALU = mybir.AluOpType

CFG = dict(D=2048, SEQ=8192, B=2, CW=1024, H=16, QL=512, KVL=512, MH=4, MEMT=256, debug=False)
EPS = 1e-6
SAME_SYNC = True
SEG = 512
PI = float(np.pi)


ALLBUFS = []


def reset_tracking():
    for b in ALLBUFS:
        b.w = {}
        b.r = {}
        b.sem = None
        b.dcount = 0


class Buf:
    def __init__(self, t, name, psum=False):
        self.t = t
        self.name = name
        self.psum = psum
        self.w = {}
        self.r = {}
        self.sem = None
        self.dcount = 0
        ALLBUFS.append(self)

    def __getitem__(self, k):
        return View(self, self.t[k])

    def v(self):
        return View(self, self.t[:])


class View:
    def __init__(self, buf, ap):
        self.buf = buf
        self.ap = ap


class Instr:
    __slots__ = ("fn", "waits", "dma")

    def __init__(self, fn, waits, dma=None):
        self.fn = fn
        self.waits = waits
        self.dma = dma


def _bufs(lst):
    out = []
    for x in lst:
        if x is None or isinstance(x, (int, float)):
            continue
        b = x.buf if isinstance(x, View) else x
        if b is not None and b not in out:
            out.append(b)
    return out


class Prog:
    ENGS = ["pe", "act", "dve", "pool", "sp"]

    NPROG = [0]

    def __init__(self, nc, stack):
        self.nc = nc
        self.stack = stack
        Prog.NPROG[0] += 1
        self.tag = "p%d" % Prog.NPROG[0]
        self.ins = {e: [] for e in self.ENGS}
        self.seen = {e: {} for e in self.ENGS}
        self.marked = {e: set() for e in self.ENGS}
        self.csem = {e: stack.enter_context(nc.semaphore(self.tag + "c_" + e)) for e in self.ENGS}
        self.dbufs = []
        self.psi = 0

    def _collect(self, eng, reads, writes, skip_dma_id=None):
        toks = []
        for b in reads:
            toks += list(b.w.values())
            if b.psum:
                toks += list(b.r.values())
        for b in writes:
            for tk in b.w.values():
                if skip_dma_id is not None and tk[0] == "d" and tk[1] == skip_dma_id:
                    continue
                toks.append(tk)
            toks += list(b.r.values())
        waits = []
        for tk in toks:
            if tk[0] == "c":
                e2, idx = tk[1], tk[2]
                if e2 == eng and (eng in ("pe", "sp") or not SAME_SYNC):
                    continue
                key, val = ("c", e2), idx
            else:
                key, val = ("d", tk[1]), tk[2]
            if self.seen[eng].get(key, -1) >= val:
                continue
            self.seen[eng][key] = val
            waits.append(tk)
            if tk[0] == "c":
                self.marked[tk[1]].add(tk[2])
        return waits

    def add(self, eng, fn, reads, writes):
        reads = _bufs(reads)
        writes = _bufs(writes)
        waits = self._collect(eng, reads, writes)
        idx = len(self.ins[eng])
        tok = ("c", eng, idx)
        for b in reads:
            b.r[eng] = tok
        for b in writes:
            b.w = {eng: tok}
            b.r = {}
        self.ins[eng].append(Instr(fn, waits))

    def dma(self, q, out, in_, sem_buf=None):
        ob, ib = out.buf, in_.buf
        sb = sem_buf or (ob if (ob is not None and not getattr(ob, "dram", False)) else ib)
        if sb is None:
            sb = ob
        if sb.sem is None:
            sb.sem = self.stack.enter_context(self.nc.semaphore(self.tag + "d_" + sb.name))
            self.dbufs.append(sb)
        reads = _bufs([ib])
        writes = _bufs([ob])
        waits = self._collect(q, reads, writes, skip_dma_id=id(sb))
        sb.dcount += 1
        tok = ("d", id(sb), sb.dcount, sb)
        key = ("d", id(sb))
        for b in reads:
            b.r[key] = tok
        for b in writes:
            b.w = {key: tok}
            b.r = {}
        oap, iap = out.ap, in_.ap
        self.ins[q].append(Instr(lambda e: e.dma_start(out=oap, in_=iap), waits, dma=sb))

    def wait_all_dma(self, eng="sp"):
        waits = []
        for b in self.dbufs:
            if b.dcount > 0:
                key = ("d", id(b))
                if self.seen[eng].get(key, -1) >= b.dcount:
                    continue
                self.seen[eng][key] = b.dcount
                waits.append(("d", id(b), b.dcount, b))
        self.ins[eng].append(Instr(lambda e: e.nop(), waits))

    def emit(self):
        rank = {}
        for e in self.ENGS:
            cnt = 0
            for i in range(len(self.ins[e])):
                if i in self.marked[e]:
                    cnt += 1
                    rank[(e, i)] = cnt
        csem = self.csem
        marked = self.marked

        def body(name):
            lst = self.ins[name]

            def f(e):
                for i, ins in enumerate(lst):
                    for tk in ins.waits:
                        if tk[0] == "c":
                            e.wait_ge(csem[tk[1]], rank[(tk[1], tk[2])])
                        else:
                            e.wait_ge(tk[3].sem, 16 * tk[2])
                    r = ins.fn(e)
                    if ins.dma is not None:
                        r.then_inc(ins.dma.sem, 16)
                    elif i in marked[name]:
                        r.then_inc(csem[name], 1)
            return f

        with self.nc.Block() as block:
            block.tensor(body("pe"))
            block.scalar(body("act"))
            block.vector(body("dve"))
            block.gpsimd(body("pool"))
            block.sync(body("sp"))

    def mm(self, out, lhsT, rhs, start=True, stop=True):
        o, l, r = out.ap, lhsT.ap, rhs.ap
        if start and out.buf.psum:
            assert not (out.buf.w and not out.buf.r), "PSUM bank %s reused before being read" % out.buf.name
        self.add("pe", lambda e: e.matmul(o, l, r, start=start, stop=stop), [lhsT, rhs], [out])

    def tr(self, out, in_, ident):
        o, i, d = out.ap, in_.ap, ident.ap
        self.add("pe", lambda e: e.transpose(o, i, d), [in_, ident], [out])

    def act(self, out, in_, func, scale=1.0, bias=0.0, accum=None):
        o, i = out.ap, in_.ap
        sc = scale.ap if isinstance(scale, View) else scale
        bi = bias.ap if isinstance(bias, View) else bias
        ac = accum.ap if accum is not None else None
        rd = [in_, scale if isinstance(scale, View) else None, bias if isinstance(bias, View) else None]
        self.add("act", lambda e: e.activation(out=o, in_=i, func=func, bias=bi, scale=sc, accum_out=ac),
                 rd, [out, accum])

    def ts(self, eng, out, in0, s1, s2, op0, op1=None):
        o, i = out.ap, in0.ap
        a1 = s1.ap if isinstance(s1, View) else s1
        a2 = s2.ap if isinstance(s2, View) else s2
        rd = [in0, s1 if isinstance(s1, View) else None, s2 if isinstance(s2, View) else None]
        if op1 is None:
            self.add(eng, lambda e: e.tensor_scalar(out=o, in0=i, scalar1=a1, scalar2=None, op0=op0), rd, [out])
        else:
            self.add(eng, lambda e: e.tensor_scalar(out=o, in0=i, scalar1=a1, scalar2=a2, op0=op0, op1=op1),
                     rd, [out])

    def tt(self, eng, out, in0, in1, op):
        o, a, b = out.ap, in0.ap, in1.ap
        self.add(eng, lambda e: e.tensor_tensor(out=o, in0=a, in1=b, op=op), [in0, in1], [out])

    def stt(self, out, in0, scalar, in1, op0, op1):
        o, a, b = out.ap, in0.ap, in1.ap
        s = scalar.ap if isinstance(scalar, View) else scalar
        self.add("dve", lambda e: e.scalar_tensor_tensor(out=o, in0=a, scalar=s, in1=b, op0=op0, op1=op1),
                 [in0, in1, scalar if isinstance(scalar, View) else None], [out])

    def copy(self, eng, out, in_):
        if eng == "act":
            self.act(out, in_, AF.Copy)
        else:
            o, i = out.ap, in_.ap
            self.add(eng, lambda e: e.tensor_copy(out=o, in_=i), [in_], [out])

    def recip(self, out, in_):
        o, i = out.ap, in_.ap
        self.add("dve", lambda e: e.reciprocal(out=o, in_=i), [in_], [out])

    def memset(self, eng, v, val):
        a = v.ap
        self.add(eng, lambda e: e.memset(a, val), [], [v])


def dims(cfg):
    d = dict(cfg)
    d["KD"] = cfg["D"] // 128
    d["NST"] = cfg["SEQ"] // SEG
    d["NSEG"] = d["NST"] // 4
    d["MW"] = cfg["MH"] * 256
    d["MLAW"] = cfg["H"] * 128
    D, CW, QL, KVL, MW, MLAW = cfg["D"], cfg["CW"], cfg["QL"], cfg["KVL"], d["MW"], d["MLAW"]
    offs = {}
    o = 0
    for nm, w in [("cg", CW), ("bg", CW), ("u", CW), ("cz", CW), ("cq", QL), ("ckv", KVL), ("kr", 64),
                  ("mz", MLAW), ("mq", MW), ("mmz", MW), ("gc", D), ("gm", D), ("gmem", D)]:
        offs[nm] = o
        o += w
    d["offs"] = offs
    d["INW"] = o
    g = {}
    c = 0
    for nm, n in [("norm", d["KD"]), ("memnorm", d["KD"]), ("qn", QL // 128), ("kvn", KVL // 128),
                  ("qnn", 1), ("knn", 1), ("qnr", 1), ("qnrsw", 1), ("knr", 1), ("knrsw", 1),
                  ("memq", 2), ("memk", 2), ("conv", 3 * (CW // 128)), ("invf", 1), ("sgn", 1)]:
        g[nm] = c
        c += n
    d["gcol"] = g
    d["NG"] = c
    return d


def build_nc(cfg):
    del ALLBUFS[:]
    d = dims(cfg)
    D, SEQ, CW, H, QL, KVL, MH, MEMT = (cfg[k] for k in ("D", "SEQ", "CW", "H", "QL", "KVL", "MH", "MEMT"))
    KD, NST, NSEG, MW, MLAW, INW, NG = (d[k] for k in ("KD", "NST", "NSEG", "MW", "MLAW", "INW", "NG"))
    offs, gcol = d["offs"], d["gcol"]
    KQ, KKV, KC, KM = QL // 128, KVL // 128, CW // 128, MW // 128
    NOWN = NSEG * SEG
    WELEMS = 4096

    nc = bass.Bass("TRN2", target_bir_lowering=False)

    def din(name, shape, dt=F32):
        return nc.dram_tensor(name, list(shape), dt, kind="ExternalInput").ap()

    xall = din("xall", [SEQ, D])
    xown = din("xown", [NOWN, D])
    xhalo = din("xhalo", [2 * NSEG, D])
    posall = din("posall", [1, SEQ], I32)
    posown = din("posown", [1, NOWN], I32)
    memb = din("memb", [MEMT, D])
    diagd = din("diagm", [128, 4 * SEG])
    abd = din("ab", [128, 8])
    identd = din("ident", [128, 128])
    gvd = din("gv", [128, NG])
    w_in = din("w_in", [D, INW])
    w_conv4 = din("w_conv4", [D, 4 * CW])
    w_krsw = din("w_krsw", [D, 64])
    w_uq = din("w_uq", [QL, H * 192])
    w_uqsw = din("w_uqsw", [QL, H * 64])
    w_ukv_k = din("w_ukv_k", [KVL, H * 128])
    w_ukv_v = din("w_ukv_v", [KVL, H * 128])
    w_conv_out = din("w_conv_out", [CW, D])
    w_mla_out = din("w_mla_out", [MLAW, D])
    w_mem_kv = din("w_mem_kv", [D, 2 * MW])
    w_mem_out = din("w_mem_out", [MW, D])
    w_o = din("w_o", [D, D])
    outd = nc.dram_tensor("out", [NOWN, D], F32, kind="ExternalOutput").ap()
    K_all = nc.dram_tensor("K_all", [128, H, SEQ], BF16, kind="Internal").ap()
    V_all = nc.dram_tensor("V_all", [128, H, SEQ // 128, 128], BF16, kind="Internal").ap()
    dbg_outs = {}
    streams = {}

    def def_stream(name, wap, nk, c0, width, ncol=None):
        ncol = ncol or (WELEMS // nk)
        nch = (width + ncol - 1) // ncol
        scr = nc.dram_tensor("wsc_" + name, [nch, 128, nk * ncol], BF16, kind="Internal").ap()
        streams[name] = dict(wap=wap, nk=nk, c0=c0, width=width, ncol=ncol, nch=nch, scr=scr)

    def_stream("conv4", w_conv4, KD, 0, 4 * CW)
    def_stream("cq", w_in, KD, offs["cq"], QL)
    def_stream("mz", w_in, KD, offs["mz"], MLAW)
    def_stream("mq", w_in, KD, offs["mq"], MW)
    def_stream("mmz", w_in, KD, offs["mmz"], MW)
    def_stream("gc", w_in, KD, offs["gc"], D)
    def_stream("gm", w_in, KD, offs["gm"], D)
    def_stream("gmem", w_in, KD, offs["gmem"], D)
    def_stream("memkv", w_mem_kv, KD, 0, 2 * MW)
    def_stream("w_conv_out", w_conv_out, KC, 0, D)
    def_stream("w_mem_out", w_mem_out, KM, 0, D)
    def_stream("w_mla_out", w_mla_out, H, 0, D)
    def_stream("w_o", w_o, KD, 0, D)
    def_stream("w_uq", w_uq, KQ, 0, H * 192, ncol=192 * max(1, (WELEMS // KQ) // 192))
    def_stream("w_uqsw", w_uqsw, KQ, 0, H * 64)

    def X(ap):
        return View(None, ap)

    def wview(wap, r0, nk, c0, ncol):
        return X(wap[r0:r0 + nk * 128, c0:c0 + ncol].rearrange("(kt p) c -> p kt c", p=128))

    with ExitStack() as gs:
        uniq = {"n": 0}

        def sb(name, shape, dt=F32, st=gs):
            uniq["n"] += 1
            nm = "s%d_%s" % (uniq["n"], name)
            return Buf(st.enter_context(nc.sbuf_tensor(nm, list(shape), dt)), nm)

        ident = sb("ident", [128, 128])
        gv = sb("gv", [128, NG])
        cm = {}
        for n in (1, 64, 128, 256, QL, KVL):
            if n not in cm:
                cm[n] = sb("cm%d" % n, [128, 128], BF16)
        kropeT = sb("kropeT", [64, SEQ], BF16)
        banks = [Buf(gs.enter_context(nc.psum_tensor("ps%d" % i, [128, 512], F32)), "ps%d" % i, psum=True)
                 for i in range(8)]
        Kd = []
        Vd = []
        for st_ in range(NST):
            kb = Buf(K_all[:, :, st_ * SEG:(st_ + 1) * SEG], "Kd%d" % st_)
            kb.dram = True
            vb = Buf(V_all[:, :, st_ * 4:(st_ + 1) * 4, :], "Vd%d" % st_)
            vb.dram = True
            Kd.append(kb)
            Vd.append(vb)

        def G(name, j=0, rows=128):
            c = gcol[name] + j
            return gv[0:rows, c:c + 1]

        state = {"psi": 0}

        def make_ps(P):
            def ps():
                b = banks[state["psi"] % 4]
                state["psi"] += 1
                return b
            return ps

        def nt_x(P, src_ap, nrows, xt, junk, ss):
            P.dma("sp", xt[0:nrows, :], X(src_ap))
            P.act(junk[0:nrows, :], xt[0:nrows, :], AF.Square, accum=ss[0:nrows, 0:1])
            P.act(ss[0:nrows, 1:2], ss[0:nrows, 0:1], AF.Sqrt, scale=1.0 / D, bias=epsc[0:nrows, 0:1])
            P.recip(ss[0:nrows, 2:3], ss[0:nrows, 1:2])
            P.ts("dve", xt[0:nrows, :], xt[0:nrows, :], ss[0:nrows, 2:3], None, ALU.mult)

        def nt_y(P, ps, nrows, xt, hT_dst, gname, col0):
            for k0 in range(0, KD, 4):
                pb = ps()
                nk = min(4, KD - k0)
                for j in range(nk):
                    kt = k0 + j
                    P.tr(pb[:, j * 128:j * 128 + nrows], xt[0:nrows, kt * 128:(kt + 1) * 128],
                         ident[0:nrows, 0:nrows])
                for j in range(nk):
                    kt = k0 + j
                    dst = hT_dst[kt][:, col0:col0 + nrows]
                    src = pb[:, j * 128:j * 128 + nrows]
                    if kt % 2 == 0:
                        P.act(dst, src, AF.Copy, scale=G(gname, kt))
                    else:
                        P.ts("dve", dst, src, G(gname, kt), None, ALU.mult)

        def norm_transpose(P, ps, src_ap, nrows, xt, junk, ss, hT_dst, gname, col0):
            nt_x(P, src_ap, nrows, xt, junk, ss)
            nt_y(P, ps, nrows, xt, hT_dst, gname, col0)

        def rope_tables(P, pos_ap, n, posi, ang, cosT, sinT, tmr):
            P.dma("sp", posi[:, 0:n], X(pos_ap.partition_broadcast(64)))
            P.copy("dve", ang[:, 0:n], posi[:, 0:n])
            P.ts("dve", ang[:, 0:n], ang[:, 0:n], G("invf", 0, 64), None, ALU.mult)
            for dst, off in ((cosT, 0.75), (sinT, 0.5)):
                P.ts("dve", dst[:, 0:n], ang[:, 0:n], 1.0 / (2 * PI), off, ALU.mult, ALU.add)
                P.copy("dve", posi[:, 0:n], dst[:, 0:n])
                P.copy("dve", tmr[:, 0:n], posi[:, 0:n])
                P.tt("dve", dst[:, 0:n], dst[:, 0:n], tmr[:, 0:n], ALU.subtract)
                P.stt(dst[:, 0:n], dst[:, 0:n], 0.0, dst[:, 0:n], ALU.is_lt, ALU.add)
                P.act(dst[:, 0:n], dst[:, 0:n], AF.Sin, scale=2 * PI, bias=negpi[0:64, 0:1])
            P.ts("dve", sinT[:, 0:n], sinT[:, 0:n], G("sgn", 0, 64), None, ALU.mult)

        def rms_feat(P, ps, srcs, n, cmat, rs, rows=128, sqb=None):
            pst = ps()
            for i, s in enumerate(srcs):
                q = sqb[i % len(sqb)]
                P.act(q[0:rows, 0:n], s, AF.Square)
                P.mm(pst[0:rows, 0:n], cmat[0:rows, 0:rows], q[0:rows, 0:n], start=(i == 0),
                     stop=(i == len(srcs) - 1))
            P.act(rs[0:rows, 0:n], pst[0:rows, 0:n], AF.Sqrt, bias=epsc[0:rows, 0:1])
            P.recip(rs[0:rows, 0:n], rs[0:rows, 0:n])

        negpi = sb("negpi", [128, 1])
        epsc = sb("epsc", [128, 1])

        with ExitStack() as s1:
            P = Prog(nc, gs)
            ps = make_ps(P)

            def t1(name, shape, dt=F32):
                return sb(name, shape, dt, st=s1)

            P.dma("sp", ident.v(), X(identd))
            P.dma("sp", gv.v(), X(gvd))
            for n, t in cm.items():
                P.memset("dve", t.v(), 1.0 / n)
            P.memset("dve", negpi.v(), -PI)
            P.memset("dve", epsc.v(), EPS)

            wkv = t1("wkv", [128, KD, KVL + 128], BF16)
            wk = t1("wk", [128, KKV, H * 128], BF16)
            wv = t1("wv", [128, KKV, H * 128], BF16)
            P.dma("pool", wkv[:, :, 0:KVL + 64], wview(w_in, 0, KD, offs["ckv"], KVL + 64))
            P.dma("pool", wkv[:, :, KVL + 64:KVL + 128], wview(w_krsw, 0, KD, 0, 64))
            P.dma("pool", wk.v(), wview(w_ukv_k, 0, KKV, 0, H * 128))
            P.dma("pool", wv.v(), wview(w_ukv_v, 0, KKV, 0, H * 128))
            wconv_sem = Buf(None, "wconv")
            for nm_, sd in streams.items():
                for j in range(sd["nch"]):
                    cw_ = min(sd["ncol"], sd["width"] - j * sd["ncol"])
                    dst = sd["scr"][j].rearrange("p (k c) -> p k c", c=sd["ncol"])[:, :, 0:cw_]
                    P.dma("pool", View(None, dst), wview(sd["wap"], 0, sd["nk"], sd["c0"] + j * sd["ncol"], cw_),
                          sem_buf=wconv_sem)

            xts = [t1("xt%d" % i, [128, D]) for i in range(2)]
            xnb = [t1("xnb%d" % i, [128, D], BF16) for i in range(2)]
            sss = [t1("ss%d" % i, [128, 4]) for i in range(2)]
            GK = min(4, KD)
            NGK = KD // GK
            hTg = [[t1("hT%d_%d" % (s_, g), [128, GK, SEG], BF16) for g in range(NGK)] for s_ in range(2)]
            identb = t1("identb", [128, 128], BF16)
            P.copy("dve", identb.v(), ident.v())

            def hTv(set_, kt):
                return hTg[set_][kt // GK][:, kt % GK, :]

            ckvf = [t1("ckvf%d" % k, [128, SEG]) for k in range(KKV)]
            ckvn = [t1("ckvn%d" % k, [128, SEG], BF16) for k in range(KKV)]
            sqb = [t1("sqb%d" % i, [128, SEG], BF16) for i in range(2)]
            kf = [t1("kf%d" % i, [128, SEG]) for i in range(4)]
            rsb = [t1("rs%d" % i, [128, SEG]) for i in range(2)]
            rsc = [t1("rsc%d" % i, [128, SEG]) for i in range(2)]
            Kst = t1("Kst", [128, H, SEG], BF16)
            Vst = t1("Vst", [128, H, 4, 128], BF16)
            posi = t1("posi", [64, SEG], I32)
            cosT = t1("cosT", [64, SEG])
            sinT = t1("sinT", [64, SEG])
            tm1 = t1("tm1", [64, SEG])
            tm2 = t1("tm2", [64, SEG])
            ang = tm1

            for kt in range(KD):
                P.ts("dve", wkv[:, kt, :], wkv[:, kt, :], G("norm", kt), None, ALU.mult)

            NTILE = 4 * NST

            def xstage(n):
                if n >= NTILE:
                    return
                i2 = n % 2
                xt, xb, ss = xts[i2], xnb[i2], sss[i2]
                P.dma("sp", xt.v(), X(xall[n * 128:(n + 1) * 128, :]))
                P.act(xb.v(), xt.v(), AF.Square, accum=ss[:, 0:1])
                P.act(ss[:, 1:2], ss[:, 0:1], AF.Sqrt, scale=1.0 / D, bias=epsc[:, 0:1])
                P.recip(ss[:, 2:3], ss[:, 1:2])
                P.ts("dve", xb.v(), xt.v(), ss[:, 2:3], None, ALU.mult)

            def ystage(n):
                if n >= NTILE:
                    return
                st_, tt_ = n // 4, n % 4
                xb = xnb[n % 2]
                for g in range(NGK):
                    pb = ps()
                    pbh = pb.t.bitcast(BF16)
                    for j in range(GK):
                        kt = g * GK + j
                        P.tr(View(pb, pbh[:, j * 128:(j + 1) * 128]), xb[:, kt * 128:(kt + 1) * 128], identb.v())
                    src = View(pb, pbh[:, 0:GK * 128].rearrange("p (k c) -> p k c", c=128))
                    P.copy("act" if g % 2 == 0 else "dve", hTg[st_ % 2][g][:, :, tt_ * 128:(tt_ + 1) * 128], src)
                xstage(n + 2)

            xstage(0)
            xstage(1)
            for n in range(4):
                ystage(n)

            def vstage(tt_):
                for h0 in range(0, H, 4):
                    nh = min(4, H - h0)
                    pb = ps()
                    for kt in range(KKV):
                        P.mm(pb[:, 0:nh * 128], ckvn[kt][:, tt_ * 128:(tt_ + 1) * 128],
                             wv[:, kt, h0 * 128:(h0 + nh) * 128], start=(kt == 0), stop=(kt == KKV - 1))
                    P.copy("act", Vst[:, h0:h0 + nh, tt_, :],
                           View(pb, pb.t[:, 0:nh * 128].rearrange("p (h d) -> p h d", d=128)))

            ins_after = [min(H + 2, (q + 1) * max(1, H // 4) - 1) for q in range(4)]
            for st_ in range(NST):
                set_ = st_ % 2
                rope_tables(P, posall[0:1, st_ * SEG:(st_ + 1) * SEG], SEG, posi, ang, cosT, sinT, tm2)
                for blk in range(KKV):
                    pb = ps()
                    for kt in range(KD):
                        P.mm(pb.v(), wkv[:, kt, blk * 128:(blk + 1) * 128], hTv(set_, kt), start=(kt == 0),
                             stop=(kt == KD - 1))
                    P.copy("act", ckvf[blk].v(), pb.v())
                rs = rsc[0]
                rms_feat(P, ps, [c.v() for c in ckvf], SEG, cm[KVL], rs, sqb=sqb)
                for blk in range(KKV):
                    P.stt(ckvn[blk].v(), ckvf[blk].v(), G("kvn", blk), rs.v(), ALU.mult, ALU.mult)
                pkr = ps()
                for kt in range(KD):
                    P.mm(pkr[0:64, :], wkv[:, kt, KVL:KVL + 64], hTv(set_, kt), start=(kt == 0), stop=(kt == KD - 1))
                pks = ps()
                for kt in range(KD):
                    P.mm(pks[0:64, :], wkv[:, kt, KVL + 64:KVL + 128], hTv(set_, kt), start=(kt == 0),
                         stop=(kt == KD - 1))
                rs = rsc[1]
                rms_feat(P, ps, [pkr[0:64, :]], SEG, cm[64], rs, rows=64, sqb=sqb)
                P.stt(tm1.v(), pkr[0:64, :], G("knr", 0, 64), cosT.v(), ALU.mult, ALU.mult)
                P.stt(tm2.v(), pks[0:64, :], G("knrsw", 0, 64), sinT.v(), ALU.mult, ALU.mult)
                P.tt("dve", tm1.v(), tm1.v(), tm2.v(), ALU.add)
                P.tt("dve", kropeT[:, st_ * SEG:(st_ + 1) * SEG], tm1.v(), rs[0:64, :], ALU.mult)
                pA = {}
                pB = {}
                for t in range(H + 3):
                    if t < H:
                        pA[t] = ps()
                        for kt in range(KKV):
                            P.mm(pA[t].v(), wk[:, kt, t * 128:(t + 1) * 128], ckvn[kt].v(), start=(kt == 0),
                                 stop=(kt == KKV - 1))
                        P.copy("act", kf[t % 4].v(), pA[t].v())
                        P.act(sqb[t % 2].v(), pA[t].v(), AF.Square)
                    h1 = t - 1
                    if 0 <= h1 < H:
                        pB[h1] = ps()
                        P.mm(pB[h1].v(), cm[128].v(), sqb[h1 % 2].v())
                        P.act(rsb[h1 % 2].v(), pB[h1].v(), AF.Sqrt, bias=epsc[:, 0:1])
                    h3 = t - 2
                    if 0 <= h3 < H:
                        P.recip(rsb[h3 % 2].v(), rsb[h3 % 2].v())
                        P.stt(Kst[:, h3, :], kf[h3 % 4].v(), G("knn"), rsb[h3 % 2].v(), ALU.mult, ALU.mult)
                    for q in range(4):
                        if ins_after[q] == t:
                            vstage(q)
                            ystage(4 * (st_ + 1) + q)
                P.dma("sp", Kd[st_].v(), Kst.v())
                P.dma("sp", Vd[st_].v(), Vst.v())
            P.wait_all_dma("sp")
            P.emit()

        reset_tracking()
        with ExitStack() as s2:
            P = Prog(nc, gs)
            ps = make_ps(P)

            def t2(name, shape, dt=F32):
                return sb(name, shape, dt, st=s2)

            accb = banks[4:8]
            diagm = t2("diagm", [128, 4 * SEG], BF16)
            abt = t2("abt", [128, 8])
            tmpm = t2("tmpm", [128, SEG], BF16)
            P.dma("pool", diagm.v(), X(diagd))
            P.dma("sp", abt.v(), X(abd))
            wbufs = [t2("wb%d" % i, [128, WELEMS], BF16) for i in range(3)]
            wmeta = [None, None, None]
            wlru = [0, 0, 0]
            wstate = {"t": 1}

            def wcols(name, c0, need=128):
                sd = streams[name]
                nk, ncol = sd["nk"], sd["ncol"]
                j = c0 // ncol
                assert (c0 + need - 1) // ncol == j
                hit = None
                for i in range(3):
                    if wmeta[i] == (name, j):
                        hit = i
                if hit is None:
                    hit = min(range(3), key=lambda i: wlru[i])
                    wb = wbufs[hit]
                    cw_ = min(ncol, sd["width"] - j * ncol)
                    if cw_ == ncol:
                        P.dma("pool", View(wb, wb.t[:, 0:nk * ncol]), View(None, sd["scr"][j]))
                    else:
                        P.dma("pool", View(wb, wb.t[:, 0:nk * ncol].rearrange("p (k c) -> p k c", c=ncol)[:, :, 0:cw_]),
                              View(None, sd["scr"][j].rearrange("p (k c) -> p k c", c=ncol)[:, :, 0:cw_]))
                    wmeta[hit] = (name, j)
                wlru[hit] = wstate["t"]
                wstate["t"] += 1
                wb = wbufs[hit]
                off = c0 - j * ncol

                def get(kt, a=0, b=need):
                    return View(wb, wb.t[:, kt * ncol + off + a:kt * ncol + off + b])
                return get

            xts = [t2("xt%d" % i, [128, D]) for i in range(2)]
            junk = t2("junk", [128, D], BF16)
            sss = [t2("ss%d" % i, [128, 4]) for i in range(2)]
            hT = [t2("hT%d" % k, [128, SEG], BF16) for k in range(KD)]
            hTh = [t2("hTh%d" % k, [128, 2 * NSEG], BF16) for k in range(KD)]
            memT = hT
            Kmem = [[t2("Kmem%d_%d" % (h, dt_), [128, MEMT], BF16) for dt_ in range(2)] for h in range(MH)]
            Vmem = [t2("Vmem%d" % mt, [128, MW], BF16) for mt in range(MEMT // 128)]
            sqb = [t2("sqb%d" % i, [128, SEG], BF16) for i in range(2)]
            f32t = [t2("f%d" % i, [128, SEG]) for i in range(6)]
            fstate = {"i": 0}

            def ftmp():
                b = f32t[fstate["i"] % len(f32t)]
                fstate["i"] += 1
                return b

            cu = t2("cu", [128, SEG + 2])
            merged = [t2("mg%d" % k, [128, SEG]) for k in range(KD)]
            actb = [t2("ab%d" % k, [128, SEG], BF16) for k in range(max(KC, KM))]
            mlab = [t2("ml%d" % k, [128, SEG], BF16) for k in range(H)]
            cqn = [t2("cqn%d" % k, [128, SEG], BF16) for k in range(KQ)]
            Qn = [t2("Qn%d" % i, [128, SEG], BF16) for i in range(2)]
            Qr = [t2("Qr%d" % i, [64, SEG], BF16) for i in range(2)]
            Kc = [t2("Kc%d" % i, [128, 2 * SEG], BF16) for i in range(3)]
            Vc = [t2("Vc%d" % i, [128, 8, 128], BF16) for i in range(3)]
            kvstate = {"i": 0}
            Pt = [t2("Pt%d" % i, [128, SEG], BF16) for i in range(4)]
            posi = t2("posi", [64, SEG], I32)
            cosT = t2("cosT", [64, SEG])
            sinT = t2("sinT", [64, SEG])
            tm1 = t2("tm1", [64, SEG])
            tm2 = t2("tm2", [64, SEG])
            ang = tm1
            mqn = [t2("mqn%d" % i, [128, SEG], BF16) for i in range(2)]

            norm_transpose(P, ps, xhalo, 2 * NSEG, xts[0], junk, sss[0], hTh, "norm", 0)

            for mt in range(MEMT // 128):
                norm_transpose(P, ps, memb[mt * 128:(mt + 1) * 128, :], 128, xts[mt % 2], junk, sss[mt % 2], memT,
                               "memnorm", mt * 128)
            for h in range(MH):
                kfs = []
                for dt_ in range(2):
                    c0 = h * 256 + dt_ * 128
                    wg_ = wcols("memkv", c0)
                    pb = ps()
                    for kt in range(KD):
                        P.mm(pb[:, 0:MEMT], wg_(kt), memT[kt][:, 0:MEMT], start=(kt == 0), stop=(kt == KD - 1))
                    fb = ftmp()
                    P.copy("act", fb[:, 0:MEMT], pb[:, 0:MEMT])
                    kfs.append(fb)
                rs = ftmp()
                rms_feat(P, ps, [k[:, 0:MEMT] for k in kfs], MEMT, cm[256], rs, sqb=sqb)
                for dt_ in range(2):
                    P.stt(Kmem[h][dt_].v(), kfs[dt_][:, 0:MEMT], G("memk", dt_), rs[:, 0:MEMT], ALU.mult, ALU.mult)
            NCW = WELEMS // KD
            for c0 in range(0, MW, NCW):
                ncw = min(NCW, MW - c0)
                wg_ = wcols("memkv", MW + c0, need=ncw)
                for mt in range(MEMT // 128):
                    pb = ps()
                    for kt in range(KD):
                        P.mm(pb[:, 0:ncw], memT[kt][:, mt * 128:(mt + 1) * 128], wg_(kt), start=(kt == 0),
                             stop=(kt == KD - 1))
                    P.copy("act", Vmem[mt][:, c0:c0 + ncw], pb[:, 0:ncw])

            def proj_fm(name, c0, rhs_list, n=SEG, halo=None):
                wg_ = wcols(name, c0)
                nk = streams[name]["nk"]
                pb = ps()
                ph = ps() if halo is not None else None
                for kt in range(nk):
                    P.mm(pb[:, 0:n], wg_(kt), rhs_list[kt][:, 0:n] if isinstance(rhs_list[kt], Buf) else rhs_list[kt],
                         start=(kt == 0), stop=(kt == nk - 1))
                    if halo is not None:
                        P.mm(ph[:, 0:2], wg_(kt), halo[kt], start=(kt == 0), stop=(kt == nk - 1))
                return pb, ph

            def out_branch(name, act_list, gname, first):
                for ob in range(KD):
                    pb, _ = proj_fm(name, ob * 128, act_list)
                    pg, _ = proj_fm(gname, ob * 128, hT)
                    sg = ftmp()
                    P.act(sg.v(), pg.v(), AF.Sigmoid)
                    if first:
                        P.tt("dve", merged[ob].v(), pb.v(), sg.v(), ALU.mult)
                    else:
                        tmp = ftmp()
                        P.tt("dve", tmp.v(), pb.v(), sg.v(), ALU.mult)
                        P.tt("dve", merged[ob].v(), merged[ob].v(), tmp.v(), ALU.add)

            scale_mla = float((128 + 64) ** -0.5)
            scale_mem = float(256 ** -0.5)

            for m in range(NSEG):
                def x2(tt_):
                    r0 = m * SEG + tt_ * 128
                    nt_x(P, xown[r0:r0 + 128, :], 128, xts[tt_ % 2], junk, sss[tt_ % 2])

                x2(0)
                x2(1)
                for tt_ in range(4):
                    nt_y(P, ps, 128, xts[tt_ % 2], hT, "norm", tt_ * 128)
                    if tt_ + 2 < 4:
                        x2(tt_ + 2)
                rope_tables(P, posown[0:1, m * SEG:(m + 1) * SEG], SEG, posi, ang, cosT, sinT, tm2)
                hl = [hTh[kt][:, 2 * m:2 * m + 2] for kt in range(KD)]

                for j in range(KC):
                    pcg, phc = proj_fm("conv4", j * 512 + 0, hT, halo=hl)
                    fcg = ftmp()
                    P.copy("act", fcg.v(), pcg.v())
                    fh = ftmp()
                    P.copy("act", fh[:, 0:2], phc[:, 0:2])
                    pu, phu = proj_fm("conv4", j * 512 + 256, hT, halo=hl)
                    P.tt("dve", cu[:, 2:SEG + 2], fcg.v(), pu.v(), ALU.mult)
                    P.tt("dve", cu[:, 0:2], fh[:, 0:2], phu[:, 0:2], ALU.mult)
                    y = ftmp()
                    P.ts("dve", y.v(), cu[:, 0:SEG], G("conv", 0 * KC + j), None, ALU.mult)
                    P.stt(y.v(), cu[:, 1:SEG + 1], G("conv", 1 * KC + j), y.v(), ALU.mult, ALU.add)
                    P.stt(y.v(), cu[:, 2:SEG + 2], G("conv", 2 * KC + j), y.v(), ALU.mult, ALU.add)
                    pbg, _ = proj_fm("conv4", j * 512 + 128, hT)
                    P.tt("dve", y.v(), y.v(), pbg.v(), ALU.mult)
                    pz, _ = proj_fm("conv4", j * 512 + 384, hT)
                    sz = ftmp()
                    P.act(sz.v(), pz.v(), AF.Silu)
                    P.tt("dve", actb[j].v(), y.v(), sz.v(), ALU.mult)
                out_branch("w_conv_out", actb[0:KC], "gc", True)

                for h in range(MH):
                    qfs = []
                    for dt_ in range(2):
                        pq, _ = proj_fm("mq", h * 256 + dt_ * 128, hT)
                        fb = ftmp()
                        P.copy("act", fb.v(), pq.v())
                        qfs.append(fb)
                    rs = ftmp()
                    rms_feat(P, ps, [q.v() for q in qfs], SEG, cm[256], rs, sqb=sqb)
                    for dt_ in range(2):
                        P.stt(mqn[dt_].v(), qfs[dt_].v(), G("memq", dt_), rs.v(), ALU.mult, ALU.mult)
                    pts = []
                    for mt in range(MEMT // 128):
                        pb = ps()
                        for dt_ in range(2):
                            P.mm(pb.v(), Kmem[h][dt_][:, mt * 128:(mt + 1) * 128], mqn[dt_].v(), start=(dt_ == 0),
                                 stop=(dt_ == 1))
                        pt = Pt[mt % 4]
                        P.act(pt.v(), pb.v(), AF.Exp, scale=scale_mem)
                        pts.append(pt)
                    nmt = MEMT // 128
                    for dvt in range(2):
                        for mt in range(nmt):
                            P.mm(accb[dvt].v(), Vmem[mt][:, h * 256 + dvt * 128:h * 256 + (dvt + 1) * 128], pts[mt].v(),
                                 start=(mt == 0), stop=(mt == nmt - 1))
                    for mt in range(nmt):
                        P.mm(accb[2].v(), cm[1].v(), pts[mt].v(), start=(mt == 0), stop=(mt == nmt - 1))
                    rinv = ftmp()
                    P.recip(rinv.v(), accb[2].v())
                    for dvt in range(2):
                        pz, _ = proj_fm("mmz", h * 256 + dvt * 128, hT)
                        sz = ftmp()
                        P.act(sz.v(), pz.v(), AF.Silu)
                        t_ = ftmp()
                        P.tt("dve", t_.v(), accb[dvt].v(), rinv.v(), ALU.mult)
                        P.tt("dve", actb[h * 2 + dvt].v(), t_.v(), sz.v(), ALU.mult)
                out_branch("w_mem_out", actb[0:KM], "gmem", False)

                cqf = []
                for blk in range(KQ):
                    pq, _ = proj_fm("cq", blk * 128, hT)
                    fb = ftmp()
                    P.copy("act", fb.v(), pq.v())
                    cqf.append(fb)
                rs = ftmp()
                rms_feat(P, ps, [q.v() for q in cqf], SEG, cm[QL], rs, sqb=sqb)
                for blk in range(KQ):
                    P.stt(cqn[blk].v(), cqf[blk].v(), G("qn", blk), rs.v(), ALU.mult, ALU.mult)
                nkc = 2 * m + 2
                nkt = nkc * 8
                LOOK = 2

                def q_prologue(h):
                    qn_b, qr_b = Qn[h % 2], Qr[h % 2]
                    wq_ = wcols("w_uq", h * 192, need=192)
                    wqs_ = wcols("w_uqsw", h * 64, need=64)
                    pb = ps()
                    for kt in range(KQ):
                        P.mm(pb.v(), wq_(kt, 0, 128), cqn[kt].v(), start=(kt == 0), stop=(kt == KQ - 1))
                    qf = ftmp()
                    P.copy("act", qf.v(), pb.v())
                    rs = ftmp()
                    rms_feat(P, ps, [qf.v()], SEG, cm[128], rs, sqb=sqb)
                    P.stt(qn_b.v(), qf.v(), G("qnn"), rs.v(), ALU.mult, ALU.mult)
                    pr = ps()
                    for kt in range(KQ):
                        P.mm(pr[0:64, :], wq_(kt, 128, 192), cqn[kt].v(), start=(kt == 0), stop=(kt == KQ - 1))
                    prs = ps()
                    for kt in range(KQ):
                        P.mm(prs[0:64, :], wqs_(kt, 0, 64), cqn[kt].v(), start=(kt == 0), stop=(kt == KQ - 1))
                    rs = ftmp()
                    rms_feat(P, ps, [pr[0:64, :]], SEG, cm[64], rs, rows=64, sqb=sqb)
                    P.stt(tm1.v(), pr[0:64, :], G("qnr", 0, 64), cosT.v(), ALU.mult, ALU.mult)
                    P.stt(tm2.v(), prs[0:64, :], G("qnrsw", 0, 64), sinT.v(), ALU.mult, ALU.mult)
                    P.tt("dve", tm1.v(), tm1.v(), tm2.v(), ALU.add)
                    P.tt("dve", qr_b.v(), tm1.v(), rs[0:64, :], ALU.mult)

                def attend(h):
                    qn_b, qr_b = Qn[h % 2], Qr[h % 2]
                    pO, pS = accb[2 * (h % 2)], accb[2 * (h % 2) + 1]
                    bufs_ = {}
                    for i in range(nkt + LOOK):
                        if i < nkt:
                            kc, k8 = i // 8, i % 8
                            if k8 == 0:
                                ci = kvstate["i"] % len(Kc)
                                kvstate["i"] += 1
                                bufs_[kc] = (Kc[ci], Vc[ci])
                                P.dma("sp", Kc[ci].v(), View(None, K_all[:, h, kc * 1024:(kc + 1) * 1024]))
                                P.dma("sp", Vc[ci].v(), View(None, V_all[:, h, kc * 8:(kc + 1) * 8, :]))
                            kcb = bufs_[kc][0]
                            pb = ps()
                            P.mm(pb.v(), kcb[:, k8 * 128:(k8 + 1) * 128], qn_b.v(), start=True, stop=False)
                            P.mm(pb.v(), kropeT[:, i * 128:(i + 1) * 128], qr_b.v(), start=False, stop=True)
                            pt = Pt[i % 4]
                            P.act(pt.v(), pb.v(), AF.Exp, scale=scale_mla)
                            if i >= 16 * m:
                                r_ = (i - 16 * m) // 4
                                ki_ = (i - 16 * m) % 4
                                P.ts("dve", tmpm.v(), diagm[:, ki_ * SEG:(ki_ + 1) * SEG], abt[:, 2 * r_:2 * r_ + 1],
                                     abt[:, 2 * r_ + 1:2 * r_ + 2], ALU.mult, ALU.add)
                                P.tt("dve", pt.v(), pt.v(), tmpm.v(), ALU.mult)
                        j = i - LOOK
                        if j >= 0:
                            vcb = bufs_[j // 8][1]
                            pt = Pt[j % 4]
                            P.mm(pO.v(), vcb[:, j % 8, :], pt.v(), start=(j == 0), stop=(j == nkt - 1))
                            P.mm(pS.v(), cm[1].v(), pt.v(), start=(j == 0), stop=(j == nkt - 1))

                def epilogue(h):
                    pO, pS = accb[2 * (h % 2)], accb[2 * (h % 2) + 1]
                    rinv = ftmp()
                    P.recip(rinv.v(), pS.v())
                    pz, _ = proj_fm("mz", h * 128, hT)
                    sz = ftmp()
                    P.act(sz.v(), pz.v(), AF.Silu)
                    t_ = ftmp()
                    P.tt("dve", t_.v(), pO.v(), rinv.v(), ALU.mult)
                    P.tt("dve", mlab[h].v(), t_.v(), sz.v(), ALU.mult)

                q_prologue(0)
                for h in range(H):
                    if h + 1 < H:
                        q_prologue(h + 1)
                    attend(h)
                    epilogue(h)
                out_branch("w_mla_out", mlab, "gm", False)

                for k in range(KD):
                    P.copy("act" if k % 2 == 0 else "dve", hT[k].v(), merged[k].v())
                NCO = WELEMS // KD
                for tt_ in range(4):
                    r0 = m * SEG + tt_ * 128
                    xr = xts[tt_ % 2]
                    P.dma("sp", xr.v(), X(xown[r0:r0 + 128, :]))
                    for c0 in range(0, D, NCO):
                        ncw = min(NCO, D - c0)
                        wo_ = wcols("w_o", c0, need=ncw)
                        pb = ps()
                        for kt in range(KD):
                            P.mm(pb[:, 0:ncw], hT[kt][:, tt_ * 128:(tt_ + 1) * 128], wo_(kt), start=(kt == 0),
                                 stop=(kt == KD - 1))
                        P.tt("dve", xr[:, c0:c0 + ncw], pb[:, 0:ncw], xr[:, c0:c0 + ncw], ALU.add)
                    P.dma("sp", View(None, outd[r0:r0 + 128, :]), xr.v())
            P.wait_all_dma("sp")
            P.emit()
    return nc


def host_inputs(cfg, inp):
    d = dims(cfg)
    D, SEQ, B, CW, H, QL, KVL, MH, MEMT = (cfg[k] for k in ("D", "SEQ", "B", "CW", "H", "QL", "KVL", "MH", "MEMT"))
    KD, NST, NSEG, MW, NG = d["KD"], d["NST"], d["NSEG"], d["MW"], d["NG"]
    offs, gcol = d["offs"], d["gcol"]
    f = lambda a: np.ascontiguousarray(np.asarray(a, dtype=np.float32))
    x = f(inp["x"])
    pos = np.ascontiguousarray(np.asarray(inp["positions"], dtype=np.int32))
    mem = f(inp["mem"])
    w_in = f(inp["w_in"][0])
    KC = CW // 128

    def cols(v, n):
        return np.asarray(v, np.float32).reshape(n, 128).T

    gv = np.zeros((128, NG), np.float32)
    gv[:, gcol["norm"]:gcol["norm"] + KD] = cols(inp["norm_g"][0], KD)
    gv[:, gcol["memnorm"]:gcol["memnorm"] + KD] = cols(inp["mem_norm_g"][0], KD)
    gv[:, gcol["qn"]:gcol["qn"] + QL // 128] = cols(inp["mla_q_norm_g"][0], QL // 128)
    gv[:, gcol["kvn"]:gcol["kvn"] + KVL // 128] = cols(inp["mla_kv_norm_g"][0], KVL // 128)
    gv[:, gcol["qnn"]] = np.asarray(inp["mla_qn_nope_g"][0], np.float32)
    gv[:, gcol["knn"]] = np.asarray(inp["mla_kn_nope_g"][0], np.float32)
    qr = np.asarray(inp["mla_qn_rope_g"][0], np.float32)
    kr = np.asarray(inp["mla_kn_rope_g"][0], np.float32)
    sw = lambda v: np.concatenate([v[32:64], v[0:32]])
    gv[0:64, gcol["qnr"]] = qr
    gv[0:64, gcol["qnrsw"]] = sw(qr)
    gv[0:64, gcol["knr"]] = kr
    gv[0:64, gcol["knrsw"]] = sw(kr)
    gv[:, gcol["memq"]:gcol["memq"] + 2] = cols(inp["mem_qn_g"][0], 2)
    gv[:, gcol["memk"]:gcol["memk"] + 2] = cols(inp["mem_kn_g"][0], 2)
    cw = np.asarray(inp["conv_w"][0], np.float32)
    for j in range(3):
        gv[:, gcol["conv"] + j * KC:gcol["conv"] + (j + 1) * KC] = cols(cw[j], KC)
    invf = np.power(np.float32(10000.0), -np.arange(32, dtype=np.float32) / np.float32(32)).astype(np.float32)
    gv[0:64, gcol["invf"]] = np.concatenate([invf, invf])
    gv[0:64, gcol["sgn"]] = np.concatenate([-np.ones(32, np.float32), np.ones(32, np.float32)])

    blocks = []
    for j in range(KC):
        for nm in ("cg", "bg", "u", "cz"):
            blocks.append(w_in[:, offs[nm] + j * 128:offs[nm] + (j + 1) * 128])
    w_conv4 = np.ascontiguousarray(np.concatenate(blocks, axis=1))
    a = offs["kr"]
    w_krsw = np.ascontiguousarray(np.concatenate([w_in[:, a + 32:a + 64], w_in[:, a:a + 32]], axis=1))
    w_uq = f(inp["w_uq"][0])
    w_uqsw = np.ascontiguousarray(np.concatenate(
        [np.concatenate([w_uq[:, h * 192 + 160:h * 192 + 192], w_uq[:, h * 192 + 128:h * 192 + 160]], axis=1)
         for h in range(H)], axis=1))
    w_ukv = f(inp["w_ukv"][0])
    w_ukv_k = np.ascontiguousarray(np.concatenate([w_ukv[:, h * 256:h * 256 + 128] for h in range(H)], axis=1))
    w_ukv_v = np.ascontiguousarray(np.concatenate([w_ukv[:, h * 256 + 128:h * 256 + 256] for h in range(H)], axis=1))
    ident = np.eye(128, dtype=np.float32)
    kk = np.arange(SEG)[:, None] // 64
    qq = np.arange(SEG)[None, :] // 64
    diag = (kk <= qq).astype(np.float32)
    diagm = np.ascontiguousarray(diag.reshape(4, 128, SEG).transpose(1, 0, 2).reshape(128, 4 * SEG))
    common = dict(ident=ident, diagm=diagm, gv=gv, w_in=w_in, w_conv4=w_conv4, w_krsw=w_krsw, w_uq=w_uq, w_uqsw=w_uqsw,
                  w_ukv_k=w_ukv_k, w_ukv_v=w_ukv_v, w_conv_out=f(inp["w_conv_out"][0]),
                  w_mla_out=f(inp["w_mla_out"][0]), w_mem_kv=f(inp["w_mem_kv"][0]),
                  w_mem_out=f(inp["w_mem_out"][0]), w_o=f(inp["w_o"][0]))
    maps = []
    for core in range(4 * B):
        b, c = core // 4, core % 4
        segs = [4 * m + c for m in range(NSEG)]
        xown = np.concatenate([x[b, s * SEG:(s + 1) * SEG] for s in segs], axis=0)
        xhalo = np.zeros((2 * NSEG, D), np.float32)
        for m, s in enumerate(segs):
            if s > 0:
                xhalo[2 * m:2 * m + 2] = x[b, s * SEG - 2:s * SEG]
        posown = np.concatenate([pos[b, s * SEG:(s + 1) * SEG] for s in segs])[None, :]
        ab = np.zeros((128, 8), np.float32)
        for r in range(4):
            if r < c:
                ab[:, 2 * r + 1] = 1.0
            elif r == c:
                ab[:, 2 * r] = 1.0
        mp = dict(common)
        mp.update(xall=np.ascontiguousarray(x[b]), xown=np.ascontiguousarray(xown), xhalo=xhalo,
                  posall=np.ascontiguousarray(pos[b][None, :]), posown=np.ascontiguousarray(posown),
                  memb=np.ascontiguousarray(mem[b]), ab=ab)
        maps.append(mp)
    return maps


def assemble(cfg, results):
    d = dims(cfg)
    D, SEQ, B = cfg["D"], cfg["SEQ"], cfg["B"]
    NSEG = d["NSEG"]
    out = np.zeros((B, SEQ, D), np.float32)
    for core in range(4 * B):
        b, c = core // 4, core % 4
        o = results[core]["out"]
        for m in range(NSEG):
            s = 4 * m + c
            out[b, s * SEG:(s + 1) * SEG] = o[m * SEG:(m + 1) * SEG]
    return out


def kernel(**inputs):
    cfg = CFG
    nc = build_nc(cfg)
    maps = host_inputs(cfg, inputs)
    res = run_bass_kernel_spmd(nc, maps, core_ids=list(range(4 * cfg["B"])))
    return assemble(cfg, res.results)
```

```python
import numpy as np
from contextlib import ExitStack
import concourse.bass as bass
import concourse.mybir as mybir
from concourse.bass_utils import run_bass_kernel_spmd

F32 = mybir.dt.float32
BF16 = mybir.dt.bfloat16
I32 = mybir.dt.int32
AF = mybir.ActivationFunctionType
ALU = mybir.AluOpType

CFG = dict(D=2048, SEQ=8192, B=2, CW=1024, H=16, QL=512, KVL=512, MH=4, MEMT=256, debug=False)
EPS = 1e-6
SAME_SYNC = True
SEG = 512
PI = float(np.pi)


ALLBUFS = []


def reset_tracking():
    for b in ALLBUFS:
        b.w = {}
        b.r = {}
        b.sem = None
        b.dcount = 0


class Buf:
    def __init__(self, t, name, psum=False):
        self.t = t
        self.name = name
        self.psum = psum
        self.w = {}
        self.r = {}
        self.sem = None
        self.dcount = 0
        ALLBUFS.append(self)

    def __getitem__(self, k):
        return View(self, self.t[k])

    def v(self):
        return View(self, self.t[:])


class View:
    def __init__(self, buf, ap):
        self.buf = buf
        self.ap = ap


class Instr:
    __slots__ = ("fn", "waits", "dma")

    def __init__(self, fn, waits, dma=None):
        self.fn = fn
        self.waits = waits
        self.dma = dma


def _bufs(lst):
    out = []
    for x in lst:
        if x is None or isinstance(x, (int, float)):
            continue
        b = x.buf if isinstance(x, View) else x
        if b is not None and b not in out:
            out.append(b)
    return out


class Prog:
    ENGS = ["pe", "act", "dve", "pool", "sp"]

    NPROG = [0]

    def __init__(self, nc, stack):
        self.nc = nc
        self.stack = stack
        Prog.NPROG[0] += 1
        self.tag = "p%d" % Prog.NPROG[0]
        self.ins = {e: [] for e in self.ENGS}
        self.seen = {e: {} for e in self.ENGS}
        self.marked = {e: set() for e in self.ENGS}
        self.csem = {e: stack.enter_context(nc.semaphore(self.tag + "c_" + e)) for e in self.ENGS}
        self.dbufs = []
        self.psi = 0

    def _collect(self, eng, reads, writes, skip_dma_id=None):
        toks = []
        for b in reads:
            toks += list(b.w.values())
            if b.psum:
                toks += list(b.r.values())
        for b in writes:
            for tk in b.w.values():
                if skip_dma_id is not None and tk[0] == "d" and tk[1] == skip_dma_id:
                    continue
                toks.append(tk)
            toks += list(b.r.values())
        waits = []
        for tk in toks:
            if tk[0] == "c":
                e2, idx = tk[1], tk[2]
                if e2 == eng and (eng in ("pe", "sp") or not SAME_SYNC):
                    continue
                key, val = ("c", e2), idx
            else:
                key, val = ("d", tk[1]), tk[2]
            if self.seen[eng].get(key, -1) >= val:
                continue
            self.seen[eng][key] = val
            waits.append(tk)
            if tk[0] == "c":
                self.marked[tk[1]].add(tk[2])
        return waits

    def add(self, eng, fn, reads, writes):
        reads = _bufs(reads)
        writes = _bufs(writes)
        waits = self._collect(eng, reads, writes)
        idx = len(self.ins[eng])
        tok = ("c", eng, idx)
        for b in reads:
            b.r[eng] = tok
        for b in writes:
            b.w = {eng: tok}
            b.r = {}
        self.ins[eng].append(Instr(fn, waits))

    def dma(self, q, out, in_, sem_buf=None):
        ob, ib = out.buf, in_.buf
        sb = sem_buf or (ob if (ob is not None and not getattr(ob, "dram", False)) else ib)
        if sb is None:
            sb = ob
        if sb.sem is None:
            sb.sem = self.stack.enter_context(self.nc.semaphore(self.tag + "d_" + sb.name))
            self.dbufs.append(sb)
        reads = _bufs([ib])
        writes = _bufs([ob])
        waits = self._collect(q, reads, writes, skip_dma_id=id(sb))
        sb.dcount += 1
        tok = ("d", id(sb), sb.dcount, sb)
        key = ("d", id(sb))
        for b in reads:
            b.r[key] = tok
        for b in writes:
            b.w = {key: tok}
            b.r = {}
        oap, iap = out.ap, in_.ap
        self.ins[q].append(Instr(lambda e: e.dma_start(out=oap, in_=iap), waits, dma=sb))

    def wait_all_dma(self, eng="sp"):
        waits = []
        for b in self.dbufs:
            if b.dcount > 0:
                key = ("d", id(b))
                if self.seen[eng].get(key, -1) >= b.dcount:
                    continue
                self.seen[eng][key] = b.dcount
                waits.append(("d", id(b), b.dcount, b))
        self.ins[eng].append(Instr(lambda e: e.nop(), waits))

    def emit(self):
        rank = {}
        for e in self.ENGS:
            cnt = 0
            for i in range(len(self.ins[e])):
                if i in self.marked[e]:
                    cnt += 1
                    rank[(e, i)] = cnt
        csem = self.csem
        marked = self.marked

        def body(name):
            lst = self.ins[name]

            def f(e):
                for i, ins in enumerate(lst):
                    for tk in ins.waits:
                        if tk[0] == "c":
                            e.wait_ge(csem[tk[1]], rank[(tk[1], tk[2])])
                        else:
                            e.wait_ge(tk[3].sem, 16 * tk[2])
                    r = ins.fn(e)
                    if ins.dma is not None:
                        r.then_inc(ins.dma.sem, 16)
                    elif i in marked[name]:
                        r.then_inc(csem[name], 1)
            return f

        with self.nc.Block() as block:
            block.tensor(body("pe"))
            block.scalar(body("act"))
            block.vector(body("dve"))
            block.gpsimd(body("pool"))
            block.sync(body("sp"))

    def mm(self, out, lhsT, rhs, start=True, stop=True):
        o, l, r = out.ap, lhsT.ap, rhs.ap
        if start and out.buf.psum:
            assert not (out.buf.w and not out.buf.r), "PSUM bank %s reused before being read" % out.buf.name
        self.add("pe", lambda e: e.matmul(o, l, r, start=start, stop=stop), [lhsT, rhs], [out])

    def tr(self, out, in_, ident):
        o, i, d = out.ap, in_.ap, ident.ap
        self.add("pe", lambda e: e.transpose(o, i, d), [in_, ident], [out])

    def act(self, out, in_, func, scale=1.0, bias=0.0, accum=None):
        o, i = out.ap, in_.ap
        sc = scale.ap if isinstance(scale, View) else scale
        bi = bias.ap if isinstance(bias, View) else bias
        ac = accum.ap if accum is not None else None
        rd = [in_, scale if isinstance(scale, View) else None, bias if isinstance(bias, View) else None]
        self.add("act", lambda e: e.activation(out=o, in_=i, func=func, bias=bi, scale=sc, accum_out=ac),
                 rd, [out, accum])

    def ts(self, eng, out, in0, s1, s2, op0, op1=None):
        o, i = out.ap, in0.ap
        a1 = s1.ap if isinstance(s1, View) else s1
        a2 = s2.ap if isinstance(s2, View) else s2
        rd = [in0, s1 if isinstance(s1, View) else None, s2 if isinstance(s2, View) else None]
        if op1 is None:
            self.add(eng, lambda e: e.tensor_scalar(out=o, in0=i, scalar1=a1, scalar2=None, op0=op0), rd, [out])
        else:
            self.add(eng, lambda e: e.tensor_scalar(out=o, in0=i, scalar1=a1, scalar2=a2, op0=op0, op1=op1),
                     rd, [out])

    def tt(self, eng, out, in0, in1, op):
        o, a, b = out.ap, in0.ap, in1.ap
        self.add(eng, lambda e: e.tensor_tensor(out=o, in0=a, in1=b, op=op), [in0, in1], [out])

    def stt(self, out, in0, scalar, in1, op0, op1):
        o, a, b = out.ap, in0.ap, in1.ap
        s = scalar.ap if isinstance(scalar, View) else scalar
        self.add("dve", lambda e: e.scalar_tensor_tensor(out=o, in0=a, scalar=s, in1=b, op0=op0, op1=op1),
                 [in0, in1, scalar if isinstance(scalar, View) else None], [out])

    def copy(self, eng, out, in_):
        if eng == "act":
            self.act(out, in_, AF.Copy)
        else:
            o, i = out.ap, in_.ap
            self.add(eng, lambda e: e.tensor_copy(out=o, in_=i), [in_], [out])

    def recip(self, out, in_):
        o, i = out.ap, in_.ap
        self.add("dve", lambda e: e.reciprocal(out=o, in_=i), [in_], [out])

    def memset(self, eng, v, val):
        a = v.ap
        self.add(eng, lambda e: e.memset(a, val), [], [v])


def dims(cfg):
    d = dict(cfg)
    d["KD"] = cfg["D"] // 128
    d["NST"] = cfg["SEQ"] // SEG
    d["NSEG"] = d["NST"] // 4
    d["MW"] = cfg["MH"] * 256
    d["MLAW"] = cfg["H"] * 128
    D, CW, QL, KVL, MW, MLAW = cfg["D"], cfg["CW"], cfg["QL"], cfg["KVL"], d["MW"], d["MLAW"]
    offs = {}
    o = 0
    for nm, w in [("cg", CW), ("bg", CW), ("u", CW), ("cz", CW), ("cq", QL), ("ckv", KVL), ("kr", 64),
                  ("mz", MLAW), ("mq", MW), ("mmz", MW), ("gc", D), ("gm", D), ("gmem", D)]:
        offs[nm] = o
        o += w
    d["offs"] = offs
    d["INW"] = o
    g = {}
    c = 0
    for nm, n in [("norm", d["KD"]), ("memnorm", d["KD"]), ("qn", QL // 128), ("kvn", KVL // 128),
                  ("qnn", 1), ("knn", 1), ("qnr", 1), ("qnrsw", 1), ("knr", 1), ("knrsw", 1),
                  ("memq", 2), ("memk", 2), ("conv", 3 * (CW // 128)), ("invf", 1), ("sgn", 1)]:
        g[nm] = c
        c += n
    d["gcol"] = g
    d["NG"] = c
    return d


def build_nc(cfg):
    del ALLBUFS[:]
    d = dims(cfg)
    D, SEQ, CW, H, QL, KVL, MH, MEMT = (cfg[k] for k in ("D", "SEQ", "CW", "H", "QL", "KVL", "MH", "MEMT"))
    KD, NST, NSEG, MW, MLAW, INW, NG = (d[k] for k in ("KD", "NST", "NSEG", "MW", "MLAW", "INW", "NG"))
    offs, gcol = d["offs"], d["gcol"]
    KQ, KKV, KC, KM = QL // 128, KVL // 128, CW // 128, MW // 128
    NOWN = NSEG * SEG
    WELEMS = 4096

    nc = bass.Bass("TRN2", target_bir_lowering=False)

    def din(name, shape, dt=F32):
        return nc.dram_tensor(name, list(shape), dt, kind="ExternalInput").ap()

    xall = din("xall", [SEQ, D])
    xown = din("xown", [NOWN, D])
    xhalo = din("xhalo", [2 * NSEG, D])
    posall = din("posall", [1, SEQ], I32)
    posown = din("posown", [1, NOWN], I32)
    memb = din("memb", [MEMT, D])
    diagd = din("diagm", [128, 4 * SEG])
    abd = din("ab", [128, 8])
    identd = din("ident", [128, 128])
    gvd = din("gv", [128, NG])
    w_in = din("w_in", [D, INW])
    w_conv4 = din("w_conv4", [D, 4 * CW])
    w_krsw = din("w_krsw", [D, 64])
    w_uq = din("w_uq", [QL, H * 192])
    w_uqsw = din("w_uqsw", [QL, H * 64])
    w_ukv_k = din("w_ukv_k", [KVL, H * 128])
    w_ukv_v = din("w_ukv_v", [KVL, H * 128])
    w_conv_out = din("w_conv_out", [CW, D])
    w_mla_out = din("w_mla_out", [MLAW, D])
    w_mem_kv = din("w_mem_kv", [D, 2 * MW])
    w_mem_out = din("w_mem_out", [MW, D])
    w_o = din("w_o", [D, D])
    outd = nc.dram_tensor("out", [NOWN, D], F32, kind="ExternalOutput").ap()
    K_all = nc.dram_tensor("K_all", [128, H, SEQ], BF16, kind="Internal").ap()
    V_all = nc.dram_tensor("V_all", [128, H, SEQ // 128, 128], BF16, kind="Internal").ap()
    dbg_outs = {}
    streams = {}

    def def_stream(name, wap, nk, c0, width, ncol=None):
        ncol = ncol or (WELEMS // nk)
        nch = (width + ncol - 1) // ncol
        scr = nc.dram_tensor("wsc_" + name, [nch, 128, nk * ncol], BF16, kind="Internal").ap()
        streams[name] = dict(wap=wap, nk=nk, c0=c0, width=width, ncol=ncol, nch=nch, scr=scr)

    def_stream("conv4", w_conv4, KD, 0, 4 * CW)
    def_stream("cq", w_in, KD, offs["cq"], QL)
    def_stream("mz", w_in, KD, offs["mz"], MLAW)
    def_stream("mq", w_in, KD, offs["mq"], MW)
    def_stream("mmz", w_in, KD, offs["mmz"], MW)
    def_stream("gc", w_in, KD, offs["gc"], D)
    def_stream("gm", w_in, KD, offs["gm"], D)
    def_stream("gmem", w_in, KD, offs["gmem"], D)
    def_stream("memkv", w_mem_kv, KD, 0, 2 * MW)
    def_stream("w_conv_out", w_conv_out, KC, 0, D)
    def_stream("w_mem_out", w_mem_out, KM, 0, D)
    def_stream("w_mla_out", w_mla_out, H, 0, D)
    def_stream("w_o", w_o, KD, 0, D)
    def_stream("w_uq", w_uq, KQ, 0, H * 192, ncol=192 * max(1, (WELEMS // KQ) // 192))
    def_stream("w_uqsw", w_uqsw, KQ, 0, H * 64)

    def X(ap):
        return View(None, ap)

    def wview(wap, r0, nk, c0, ncol):
        return X(wap[r0:r0 + nk * 128, c0:c0 + ncol].rearrange("(kt p) c -> p kt c", p=128))

    with ExitStack() as gs:
        uniq = {"n": 0}

        def sb(name, shape, dt=F32, st=gs):
            uniq["n"] += 1
            nm = "s%d_%s" % (uniq["n"], name)
            return Buf(st.enter_context(nc.sbuf_tensor(nm, list(shape), dt)), nm)

        ident = sb("ident", [128, 128])
        gv = sb("gv", [128, NG])
        cm = {}
        for n in (1, 64, 128, 256, QL, KVL):
            if n not in cm:
                cm[n] = sb("cm%d" % n, [128, 128], BF16)
        kropeT = sb("kropeT", [64, SEQ], BF16)
        banks = [Buf(gs.enter_context(nc.psum_tensor("ps%d" % i, [128, 512], F32)), "ps%d" % i, psum=True)
                 for i in range(8)]
        Kd = []
        Vd = []
        for st_ in range(NST):
            kb = Buf(K_all[:, :, st_ * SEG:(st_ + 1) * SEG], "Kd%d" % st_)
            kb.dram = True
            vb = Buf(V_all[:, :, st_ * 4:(st_ + 1) * 4, :], "Vd%d" % st_)
            vb.dram = True
            Kd.append(kb)
            Vd.append(vb)

        def G(name, j=0, rows=128):
            c = gcol[name] + j
            return gv[0:rows, c:c + 1]

        state = {"psi": 0}

        def make_ps(P):
            def ps():
                b = banks[state["psi"] % 4]
                state["psi"] += 1
                return b
            return ps

        def nt_x(P, src_ap, nrows, xt, junk, ss):
            P.dma("sp", xt[0:nrows, :], X(src_ap))
            P.act(junk[0:nrows, :], xt[0:nrows, :], AF.Square, accum=ss[0:nrows, 0:1])
            P.act(ss[0:nrows, 1:2], ss[0:nrows, 0:1], AF.Ln, scale=1.0 / D, bias=epsc[0:nrows, 0:1])
            P.act(ss[0:nrows, 2:3], ss[0:nrows, 1:2], AF.Exp, scale=-0.5)
            P.ts("dve", xt[0:nrows, :], xt[0:nrows, :], ss[0:nrows, 2:3], None, ALU.mult)

        def nt_y(P, ps, nrows, xt, hT_dst, gname, col0):
            for k0 in range(0, KD, 4):
                pb = ps()
                nk = min(4, KD - k0)
                for j in range(nk):
                    kt = k0 + j
                    P.tr(pb[:, j * 128:j * 128 + nrows], xt[0:nrows, kt * 128:(kt + 1) * 128],
                         ident[0:nrows, 0:nrows])
                for j in range(nk):
                    kt = k0 + j
                    dst = hT_dst[kt][:, col0:col0 + nrows]
                    src = pb[:, j * 128:j * 128 + nrows]
                    if kt % 2 == 0:
                        P.act(dst, src, AF.Copy, scale=G(gname, kt))
                    else:
                        P.ts("dve", dst, src, G(gname, kt), None, ALU.mult)

        def norm_transpose(P, ps, src_ap, nrows, xt, junk, ss, hT_dst, gname, col0):
            nt_x(P, src_ap, nrows, xt, junk, ss)
            nt_y(P, ps, nrows, xt, hT_dst, gname, col0)

        def rope_tables(P, pos_ap, n, posi, ang, cosT, sinT, tmr):
            P.dma("sp", posi[:, 0:n], X(pos_ap.partition_broadcast(64)))
            P.copy("dve", ang[:, 0:n], posi[:, 0:n])
            P.ts("dve", ang[:, 0:n], ang[:, 0:n], G("invf", 0, 64), None, ALU.mult)
            for dst, off in ((cosT, 0.75), (sinT, 0.5)):
                P.ts("dve", dst[:, 0:n], ang[:, 0:n], 1.0 / (2 * PI), off, ALU.mult, ALU.add)
                P.copy("dve", posi[:, 0:n], dst[:, 0:n])
                P.copy("dve", tmr[:, 0:n], posi[:, 0:n])
                P.tt("dve", dst[:, 0:n], dst[:, 0:n], tmr[:, 0:n], ALU.subtract)
                P.stt(dst[:, 0:n], dst[:, 0:n], 0.0, dst[:, 0:n], ALU.is_lt, ALU.add)
                P.act(dst[:, 0:n], dst[:, 0:n], AF.Sin, scale=2 * PI, bias=negpi[0:64, 0:1])
            P.ts("dve", sinT[:, 0:n], sinT[:, 0:n], G("sgn", 0, 64), None, ALU.mult)

        def rms_feat(P, ps, srcs, n, cmat, rs, rows=128, sqb=None):
            pst = ps()
            for i, s in enumerate(srcs):
                q = sqb[i % len(sqb)]
                P.act(q[0:rows, 0:n], s, AF.Square)
                P.mm(pst[0:rows, 0:n], cmat[0:rows, 0:rows], q[0:rows, 0:n], start=(i == 0),
                     stop=(i == len(srcs) - 1))
            P.act(rs[0:rows, 0:n], pst[0:rows, 0:n], AF.Ln, bias=epsc[0:rows, 0:1])
            P.act(rs[0:rows, 0:n], rs[0:rows, 0:n], AF.Exp, scale=-0.5)

        negpi = sb("negpi", [128, 1])
        epsc = sb("epsc", [128, 1])

        with ExitStack() as s1:
            P = Prog(nc, gs)
            ps = make_ps(P)

            def t1(name, shape, dt=F32):
                return sb(name, shape, dt, st=s1)

            P.dma("sp", ident.v(), X(identd))
            P.dma("sp", gv.v(), X(gvd))
            for n, t in cm.items():
                P.memset("dve", t.v(), 1.0 / n)
            P.memset("dve", negpi.v(), -PI)
            P.memset("dve", epsc.v(), EPS)

            wkv = t1("wkv", [128, KD, KVL + 128], BF16)
            wk = t1("wk", [128, KKV, H * 128], BF16)
            wv = t1("wv", [128, KKV, H * 128], BF16)
            P.dma("pool", wkv[:, :, 0:KVL + 64], wview(w_in, 0, KD, offs["ckv"], KVL + 64))
            P.dma("pool", wkv[:, :, KVL + 64:KVL + 128], wview(w_krsw, 0, KD, 0, 64))
            P.dma("pool", wk.v(), wview(w_ukv_k, 0, KKV, 0, H * 128))
            P.dma("pool", wv.v(), wview(w_ukv_v, 0, KKV, 0, H * 128))
            wconv_sem = Buf(None, "wconv")
            for nm_, sd in streams.items():
                for j in range(sd["nch"]):
                    cw_ = min(sd["ncol"], sd["width"] - j * sd["ncol"])
                    dst = sd["scr"][j].rearrange("p (k c) -> p k c", c=sd["ncol"])[:, :, 0:cw_]
                    P.dma("pool", View(None, dst), wview(sd["wap"], 0, sd["nk"], sd["c0"] + j * sd["ncol"], cw_),
                          sem_buf=wconv_sem)

            xts = [t1("xt%d" % i, [128, D]) for i in range(2)]
            xnb = [t1("xnb%d" % i, [128, D], BF16) for i in range(2)]
            sss = [t1("ss%d" % i, [128, 4]) for i in range(2)]
            GK = min(4, KD)
            NGK = KD // GK
            hTg = [[t1("hT%d_%d" % (s_, g), [128, GK, SEG], BF16) for g in range(NGK)] for s_ in range(2)]
            identb = t1("identb", [128, 128], BF16)
            P.copy("dve", identb.v(), ident.v())

            def hTv(set_, kt):
                return hTg[set_][kt // GK][:, kt % GK, :]

            ckvf = [t1("ckvf%d" % k, [128, SEG]) for k in range(KKV)]
            ckvn = [t1("ckvn%d" % k, [128, SEG], BF16) for k in range(KKV)]
            sqb = [t1("sqb%d" % i, [128, SEG], BF16) for i in range(2)]
            kf = [t1("kf%d" % i, [128, SEG]) for i in range(4)]
            rsb = [t1("rs%d" % i, [128, SEG]) for i in range(2)]
            rsc = [t1("rsc%d" % i, [128, SEG]) for i in range(2)]
            Kst = t1("Kst", [128, H, SEG], BF16)
            Vst = t1("Vst", [128, H, 4, 128], BF16)
            posi = t1("posi", [64, SEG], I32)
            cosT = t1("cosT", [64, SEG])
            sinT = t1("sinT", [64, SEG])
            tm1 = t1("tm1", [64, SEG])
            tm2 = t1("tm2", [64, SEG])
            ang = tm1

            for kt in range(KD):
                P.ts("dve", wkv[:, kt, :], wkv[:, kt, :], G("norm", kt), None, ALU.mult)

            NTILE = 4 * NST

            def xstage(n):
                if n >= NTILE:
                    return
                i2 = n % 2
                xt, xb, ss = xts[i2], xnb[i2], sss[i2]
                P.dma("sp", xt.v(), X(xall[n * 128:(n + 1) * 128, :]))
                P.act(xb.v(), xt.v(), AF.Square, accum=ss[:, 0:1])
                P.act(ss[:, 1:2], ss[:, 0:1], AF.Ln, scale=1.0 / D, bias=epsc[:, 0:1])
                P.act(ss[:, 2:3], ss[:, 1:2], AF.Exp, scale=-0.5)
                P.ts("dve", xb.v(), xt.v(), ss[:, 2:3], None, ALU.mult)

            def ystage(n):
                if n >= NTILE:
                    return
                st_, tt_ = n // 4, n % 4
                xb = xnb[n % 2]
                for g in range(NGK):
                    pb = ps()
                    pbh = pb.t.bitcast(BF16)
                    for j in range(GK):
                        kt = g * GK + j
                        P.tr(View(pb, pbh[:, j * 128:(j + 1) * 128]), xb[:, kt * 128:(kt + 1) * 128], identb.v())
                    src = View(pb, pbh[:, 0:GK * 128].rearrange("p (k c) -> p k c", c=128))
                    P.copy("act" if g % 2 == 0 else "dve", hTg[st_ % 2][g][:, :, tt_ * 128:(tt_ + 1) * 128], src)
                xstage(n + 2)

            xstage(0)
            xstage(1)
            for n in range(4):
                ystage(n)

            def vstage(tt_):
                for h0 in range(0, H, 4):
                    nh = min(4, H - h0)
                    pb = ps()
                    for kt in range(KKV):
                        P.mm(pb[:, 0:nh * 128], ckvn[kt][:, tt_ * 128:(tt_ + 1) * 128],
                             wv[:, kt, h0 * 128:(h0 + nh) * 128], start=(kt == 0), stop=(kt == KKV - 1))
                    P.copy("act", Vst[:, h0:h0 + nh, tt_, :],
                           View(pb, pb.t[:, 0:nh * 128].rearrange("p (h d) -> p h d", d=128)))

            ins_after = [min(H + 2, (q + 1) * max(1, H // 4) - 1) for q in range(4)]
            for st_ in range(NST):
                set_ = st_ % 2
                rope_tables(P, posall[0:1, st_ * SEG:(st_ + 1) * SEG], SEG, posi, ang, cosT, sinT, tm2)
                for blk in range(KKV):
                    pb = ps()
                    for kt in range(KD):
                        P.mm(pb.v(), wkv[:, kt, blk * 128:(blk + 1) * 128], hTv(set_, kt), start=(kt == 0),
                             stop=(kt == KD - 1))
                    P.copy("act", ckvf[blk].v(), pb.v())
                rs = rsc[0]
                rms_feat(P, ps, [c.v() for c in ckvf], SEG, cm[KVL], rs, sqb=sqb)
                for blk in range(KKV):
                    P.stt(ckvn[blk].v(), ckvf[blk].v(), G("kvn", blk), rs.v(), ALU.mult, ALU.mult)
                pkr = ps()
                for kt in range(KD):
                    P.mm(pkr[0:64, :], wkv[:, kt, KVL:KVL + 64], hTv(set_, kt), start=(kt == 0), stop=(kt == KD - 1))
                pks = ps()
                for kt in range(KD):
                    P.mm(pks[0:64, :], wkv[:, kt, KVL + 64:KVL + 128], hTv(set_, kt), start=(kt == 0),
                         stop=(kt == KD - 1))
                rs = rsc[1]
                rms_feat(P, ps, [pkr[0:64, :]], SEG, cm[64], rs, rows=64, sqb=sqb)
                P.stt(tm1.v(), pkr[0:64, :], G("knr", 0, 64), cosT.v(), ALU.mult, ALU.mult)
                P.stt(tm2.v(), pks[0:64, :], G("knrsw", 0, 64), sinT.v(), ALU.mult, ALU.mult)
                P.tt("dve", tm1.v(), tm1.v(), tm2.v(), ALU.add)
                P.tt("dve", kropeT[:, st_ * SEG:(st_ + 1) * SEG], tm1.v(), rs[0:64, :], ALU.mult)
                pA = {}
                pB = {}
                for t in range(H + 3):
                    if t < H:
                        pA[t] = ps()
                        for kt in range(KKV):
                            P.mm(pA[t].v(), wk[:, kt, t * 128:(t + 1) * 128], ckvn[kt].v(), start=(kt == 0),
                                 stop=(kt == KKV - 1))
                        P.copy("act", kf[t % 4].v(), pA[t].v())
                        P.act(sqb[t % 2].v(), pA[t].v(), AF.Square)
                    h1 = t - 1
                    if 0 <= h1 < H:
                        pB[h1] = ps()
                        P.mm(pB[h1].v(), cm[128].v(), sqb[h1 % 2].v())
                        P.act(rsb[h1 % 2].v(), pB[h1].v(), AF.Ln, bias=epsc[:, 0:1])
                        P.act(rsb[h1 % 2].v(), rsb[h1 % 2].v(), AF.Exp, scale=-0.5)
                    h3 = t - 2
                    if 0 <= h3 < H:
                        P.stt(Kst[:, h3, :], kf[h3 % 4].v(), G("knn"), rsb[h3 % 2].v(), ALU.mult, ALU.mult)
                    for q in range(4):
                        if ins_after[q] == t:
                            vstage(q)
                            ystage(4 * (st_ + 1) + q)
                P.dma("sp", Kd[st_].v(), Kst.v())
                P.dma("sp", Vd[st_].v(), Vst.v())
            P.wait_all_dma("sp")
            P.emit()

        reset_tracking()
        with ExitStack() as s2:
            P = Prog(nc, gs)
            ps = make_ps(P)

            def t2(name, shape, dt=F32):
                return sb(name, shape, dt, st=s2)

            accb = banks[4:8]
            diagm = t2("diagm", [128, 4 * SEG], BF16)
            abt = t2("abt", [128, 8])
            tmpm = t2("tmpm", [128, SEG], BF16)
            P.dma("pool", diagm.v(), X(diagd))
            P.dma("sp", abt.v(), X(abd))
            wbufs = [t2("wb%d" % i, [128, WELEMS], BF16) for i in range(3)]
            wmeta = [None, None, None]
            wlru = [0, 0, 0]
            wstate = {"t": 1}

            def wcols(name, c0, need=128):
                sd = streams[name]
                nk, ncol = sd["nk"], sd["ncol"]
                j = c0 // ncol
                assert (c0 + need - 1) // ncol == j
                hit = None
                for i in range(3):
                    if wmeta[i] == (name, j):
                        hit = i
                if hit is None:
                    hit = min(range(3), key=lambda i: wlru[i])
                    wb = wbufs[hit]
                    cw_ = min(ncol, sd["width"] - j * ncol)
                    if cw_ == ncol:
                        P.dma("pool", View(wb, wb.t[:, 0:nk * ncol]), View(None, sd["scr"][j]))
                    else:
                        P.dma("pool", View(wb, wb.t[:, 0:nk * ncol].rearrange("p (k c) -> p k c", c=ncol)[:, :, 0:cw_]),
                              View(None, sd["scr"][j].rearrange("p (k c) -> p k c", c=ncol)[:, :, 0:cw_]))
                    wmeta[hit] = (name, j)
                wlru[hit] = wstate["t"]
                wstate["t"] += 1
                wb = wbufs[hit]
                off = c0 - j * ncol

                def get(kt, a=0, b=need):
                    return View(wb, wb.t[:, kt * ncol + off + a:kt * ncol + off + b])
                return get

            xts = [t2("xt%d" % i, [128, D]) for i in range(2)]
            junk = t2("junk", [128, D], BF16)
            sss = [t2("ss%d" % i, [128, 4]) for i in range(2)]
            hT = [t2("hT%d" % k, [128, SEG], BF16) for k in range(KD)]
            hTh = [t2("hTh%d" % k, [128, 2 * NSEG], BF16) for k in range(KD)]
            memT = hT
            Kmem = [[t2("Kmem%d_%d" % (h, dt_), [128, MEMT], BF16) for dt_ in range(2)] for h in range(MH)]
            Vmem = [t2("Vmem%d" % mt, [128, MW], BF16) for mt in range(MEMT // 128)]
            sqb = [t2("sqb%d" % i, [128, SEG], BF16) for i in range(2)]
            f32t = [t2("f%d" % i, [128, SEG]) for i in range(6)]
            fstate = {"i": 0}

            def ftmp():
                b = f32t[fstate["i"] % len(f32t)]
                fstate["i"] += 1
                return b

            cu = t2("cu", [128, SEG + 2])
            merged = [t2("mg%d" % k, [128, SEG]) for k in range(KD)]
            actb = [t2("ab%d" % k, [128, SEG], BF16) for k in range(max(KC, KM))]
            mlab = [t2("ml%d" % k, [128, SEG], BF16) for k in range(H)]
            cqn = [t2("cqn%d" % k, [128, SEG], BF16) for k in range(KQ)]
            Qn = [t2("Qn%d" % i, [128, SEG], BF16) for i in range(2)]
            Qr = [t2("Qr%d" % i, [64, SEG], BF16) for i in range(2)]
            Kc = [t2("Kc%d" % i, [128, 2 * SEG], BF16) for i in range(3)]
            Vc = [t2("Vc%d" % i, [128, 8, 128], BF16) for i in range(3)]
            kvstate = {"i": 0}
            Pt = [t2("Pt%d" % i, [128, SEG], BF16) for i in range(4)]
            posi = t2("posi", [64, SEG], I32)
            cosT = t2("cosT", [64, SEG])
            sinT = t2("sinT", [64, SEG])
            tm1 = t2("tm1", [64, SEG])
            tm2 = t2("tm2", [64, SEG])
            ang = tm1
            mqn = [t2("mqn%d" % i, [128, SEG], BF16) for i in range(2)]

            norm_transpose(P, ps, xhalo, 2 * NSEG, xts[0], junk, sss[0], hTh, "norm", 0)

            for mt in range(MEMT // 128):
                norm_transpose(P, ps, memb[mt * 128:(mt + 1) * 128, :], 128, xts[mt % 2], junk, sss[mt % 2], memT,
                               "memnorm", mt * 128)
            for h in range(MH):
                kfs = []
                for dt_ in range(2):
                    c0 = h * 256 + dt_ * 128
                    wg_ = wcols("memkv", c0)
                    pb = ps()
                    for kt in range(KD):
                        P.mm(pb[:, 0:MEMT], wg_(kt), memT[kt][:, 0:MEMT], start=(kt == 0), stop=(kt == KD - 1))
                    fb = ftmp()
                    P.copy("act", fb[:, 0:MEMT], pb[:, 0:MEMT])
                    kfs.append(fb)
                rs = ftmp()
                rms_feat(P, ps, [k[:, 0:MEMT] for k in kfs], MEMT, cm[256], rs, sqb=sqb)
                for dt_ in range(2):
                    P.stt(Kmem[h][dt_].v(), kfs[dt_][:, 0:MEMT], G("memk", dt_), rs[:, 0:MEMT], ALU.mult, ALU.mult)
            NCW = WELEMS // KD
            for c0 in range(0, MW, NCW):
                ncw = min(NCW, MW - c0)
                wg_ = wcols("memkv", MW + c0, need=ncw)
                for mt in range(MEMT // 128):
                    pb = ps()
                    for kt in range(KD):
                        P.mm(pb[:, 0:ncw], memT[kt][:, mt * 128:(mt + 1) * 128], wg_(kt), start=(kt == 0),
                             stop=(kt == KD - 1))
                    P.copy("act", Vmem[mt][:, c0:c0 + ncw], pb[:, 0:ncw])

            def proj_fm(name, c0, rhs_list, n=SEG, halo=None):
                wg_ = wcols(name, c0)
                nk = streams[name]["nk"]
                pb = ps()
                ph = ps() if halo is not None else None
                for kt in range(nk):
                    P.mm(pb[:, 0:n], wg_(kt), rhs_list[kt][:, 0:n] if isinstance(rhs_list[kt], Buf) else rhs_list[kt],
                         start=(kt == 0), stop=(kt == nk - 1))
                    if halo is not None:
                        P.mm(ph[:, 0:2], wg_(kt), halo[kt], start=(kt == 0), stop=(kt == nk - 1))
                return pb, ph

            def out_branch(name, act_list, gname, first):
                for ob in range(KD):
                    pb, _ = proj_fm(name, ob * 128, act_list)
                    pg, _ = proj_fm(gname, ob * 128, hT)
                    sg = ftmp()
                    P.act(sg.v(), pg.v(), AF.Sigmoid)
                    if first:
                        P.tt("dve", merged[ob].v(), pb.v(), sg.v(), ALU.mult)
                    else:
                        tmp = ftmp()
                        P.tt("dve", tmp.v(), pb.v(), sg.v(), ALU.mult)
                        P.tt("dve", merged[ob].v(), merged[ob].v(), tmp.v(), ALU.add)

            scale_mla = float((128 + 64) ** -0.5)
            scale_mem = float(256 ** -0.5)

            for m in range(NSEG):
                def x2(tt_):
                    r0 = m * SEG + tt_ * 128
                    nt_x(P, xown[r0:r0 + 128, :], 128, xts[tt_ % 2], junk, sss[tt_ % 2])

                x2(0)
                x2(1)
                for tt_ in range(4):
                    nt_y(P, ps, 128, xts[tt_ % 2], hT, "norm", tt_ * 128)
                    if tt_ + 2 < 4:
                        x2(tt_ + 2)
                rope_tables(P, posown[0:1, m * SEG:(m + 1) * SEG], SEG, posi, ang, cosT, sinT, tm2)
                hl = [hTh[kt][:, 2 * m:2 * m + 2] for kt in range(KD)]

                for j in range(KC):
                    pcg, phc = proj_fm("conv4", j * 512 + 0, hT, halo=hl)
                    fcg = ftmp()
                    P.copy("act", fcg.v(), pcg.v())
                    fh = ftmp()
                    P.copy("act", fh[:, 0:2], phc[:, 0:2])
                    pu, phu = proj_fm("conv4", j * 512 + 256, hT, halo=hl)
                    P.tt("dve", cu[:, 2:SEG + 2], fcg.v(), pu.v(), ALU.mult)
                    P.tt("dve", cu[:, 0:2], fh[:, 0:2], phu[:, 0:2], ALU.mult)
                    y = ftmp()
                    P.ts("dve", y.v(), cu[:, 0:SEG], G("conv", 0 * KC + j), None, ALU.mult)
                    P.stt(y.v(), cu[:, 1:SEG + 1], G("conv", 1 * KC + j), y.v(), ALU.mult, ALU.add)
                    P.stt(y.v(), cu[:, 2:SEG + 2], G("conv", 2 * KC + j), y.v(), ALU.mult, ALU.add)
                    pbg, _ = proj_fm("conv4", j * 512 + 128, hT)
                    P.tt("dve", y.v(), y.v(), pbg.v(), ALU.mult)
                    pz, _ = proj_fm("conv4", j * 512 + 384, hT)
                    sz = ftmp()
                    P.act(sz.v(), pz.v(), AF.Silu)
                    P.tt("dve", actb[j].v(), y.v(), sz.v(), ALU.mult)
                out_branch("w_conv_out", actb[0:KC], "gc", True)

                for h in range(MH):
                    qfs = []
                    for dt_ in range(2):
                        pq, _ = proj_fm("mq", h * 256 + dt_ * 128, hT)
                        fb = ftmp()
                        P.copy("act", fb.v(), pq.v())
                        qfs.append(fb)
                    rs = ftmp()
                    rms_feat(P, ps, [q.v() for q in qfs], SEG, cm[256], rs, sqb=sqb)
                    for dt_ in range(2):
                        P.stt(mqn[dt_].v(), qfs[dt_].v(), G("memq", dt_), rs.v(), ALU.mult, ALU.mult)
                    pts = []
                    for mt in range(MEMT // 128):
                        pb = ps()
                        for dt_ in range(2):
                            P.mm(pb.v(), Kmem[h][dt_][:, mt * 128:(mt + 1) * 128], mqn[dt_].v(), start=(dt_ == 0),
                                 stop=(dt_ == 1))
                        pt = Pt[mt % 4]
                        P.act(pt.v(), pb.v(), AF.Exp, scale=scale_mem)
                        pts.append(pt)
                    nmt = MEMT // 128
                    for dvt in range(2):
                        for mt in range(nmt):
                            P.mm(accb[dvt].v(), Vmem[mt][:, h * 256 + dvt * 128:h * 256 + (dvt + 1) * 128], pts[mt].v(),
                                 start=(mt == 0), stop=(mt == nmt - 1))
                    for mt in range(nmt):
                        P.mm(accb[2].v(), cm[1].v(), pts[mt].v(), start=(mt == 0), stop=(mt == nmt - 1))
                    rinv = ftmp()
                    P.act(rinv.v(), accb[2].v(), AF.Ln)
                    P.act(rinv.v(), rinv.v(), AF.Exp, scale=-1.0)
                    for dvt in range(2):
                        pz, _ = proj_fm("mmz", h * 256 + dvt * 128, hT)
                        sz = ftmp()
                        P.act(sz.v(), pz.v(), AF.Silu)
                        t_ = ftmp()
                        P.tt("dve", t_.v(), accb[dvt].v(), rinv.v(), ALU.mult)
                        P.tt("dve", actb[h * 2 + dvt].v(), t_.v(), sz.v(), ALU.mult)
                out_branch("w_mem_out", actb[0:KM], "gmem", False)

                cqf = []
                for blk in range(KQ):
                    pq, _ = proj_fm("cq", blk * 128, hT)
                    fb = ftmp()
                    P.copy("act", fb.v(), pq.v())
                    cqf.append(fb)
                rs = ftmp()
                rms_feat(P, ps, [q.v() for q in cqf], SEG, cm[QL], rs, sqb=sqb)
                for blk in range(KQ):
                    P.stt(cqn[blk].v(), cqf[blk].v(), G("qn", blk), rs.v(), ALU.mult, ALU.mult)
                nkc = 2 * m + 2
                nkt = nkc * 8
                LOOK = 2

                def q_prologue(h):
                    qn_b, qr_b = Qn[h % 2], Qr[h % 2]
                    wq_ = wcols("w_uq", h * 192, need=192)
                    wqs_ = wcols("w_uqsw", h * 64, need=64)
                    pb = ps()
                    for kt in range(KQ):
                        P.mm(pb.v(), wq_(kt, 0, 128), cqn[kt].v(), start=(kt == 0), stop=(kt == KQ - 1))
                    qf = ftmp()
                    P.copy("act", qf.v(), pb.v())
                    rs = ftmp()
                    rms_feat(P, ps, [qf.v()], SEG, cm[128], rs, sqb=sqb)
                    P.stt(qn_b.v(), qf.v(), G("qnn"), rs.v(), ALU.mult, ALU.mult)
                    pr = ps()
                    for kt in range(KQ):
                        P.mm(pr[0:64, :], wq_(kt, 128, 192), cqn[kt].v(), start=(kt == 0), stop=(kt == KQ - 1))
                    prs = ps()
                    for kt in range(KQ):
                        P.mm(prs[0:64, :], wqs_(kt, 0, 64), cqn[kt].v(), start=(kt == 0), stop=(kt == KQ - 1))
                    rs = ftmp()
                    rms_feat(P, ps, [pr[0:64, :]], SEG, cm[64], rs, rows=64, sqb=sqb)
                    P.stt(tm1.v(), pr[0:64, :], G("qnr", 0, 64), cosT.v(), ALU.mult, ALU.mult)
                    P.stt(tm2.v(), prs[0:64, :], G("qnrsw", 0, 64), sinT.v(), ALU.mult, ALU.mult)
                    P.tt("dve", tm1.v(), tm1.v(), tm2.v(), ALU.add)
                    P.tt("dve", qr_b.v(), tm1.v(), rs[0:64, :], ALU.mult)

                def attend(h):
                    qn_b, qr_b = Qn[h % 2], Qr[h % 2]
                    pO, pS = accb[2 * (h % 2)], accb[2 * (h % 2) + 1]
                    bufs_ = {}
                    for i in range(nkt + LOOK):
                        if i < nkt:
                            kc, k8 = i // 8, i % 8
                            if k8 == 0:
                                ci = kvstate["i"] % len(Kc)
                                kvstate["i"] += 1
                                bufs_[kc] = (Kc[ci], Vc[ci])
                                P.dma("sp", Kc[ci].v(), View(None, K_all[:, h, kc * 1024:(kc + 1) * 1024]))
                                P.dma("sp", Vc[ci].v(), View(None, V_all[:, h, kc * 8:(kc + 1) * 8, :]))
                            kcb = bufs_[kc][0]
                            pb = ps()
                            P.mm(pb.v(), kcb[:, k8 * 128:(k8 + 1) * 128], qn_b.v(), start=True, stop=False)
                            P.mm(pb.v(), kropeT[:, i * 128:(i + 1) * 128], qr_b.v(), start=False, stop=True)
                            pt = Pt[i % 4]
                            P.act(pt.v(), pb.v(), AF.Exp, scale=scale_mla)
                            if i >= 16 * m:
                                r_ = (i - 16 * m) // 4
                                ki_ = (i - 16 * m) % 4
                                P.ts("dve", tmpm.v(), diagm[:, ki_ * SEG:(ki_ + 1) * SEG], abt[:, 2 * r_:2 * r_ + 1],
                                     abt[:, 2 * r_ + 1:2 * r_ + 2], ALU.mult, ALU.add)
                                P.tt("dve", pt.v(), pt.v(), tmpm.v(), ALU.mult)
                        j = i - LOOK
                        if j >= 0:
                            vcb = bufs_[j // 8][1]
                            pt = Pt[j % 4]
                            P.mm(pO.v(), vcb[:, j % 8, :], pt.v(), start=(j == 0), stop=(j == nkt - 1))
                            P.mm(pS.v(), cm[1].v(), pt.v(), start=(j == 0), stop=(j == nkt - 1))

                def epilogue(h):
                    pO, pS = accb[2 * (h % 2)], accb[2 * (h % 2) + 1]
                    rinv = ftmp()
                    P.act(rinv.v(), pS.v(), AF.Ln)
                    P.act(rinv.v(), rinv.v(), AF.Exp, scale=-1.0)
                    pz, _ = proj_fm("mz", h * 128, hT)
                    sz = ftmp()
                    P.act(sz.v(), pz.v(), AF.Silu)
                    t_ = ftmp()
                    P.tt("dve", t_.v(), pO.v(), rinv.v(), ALU.mult)
                    P.tt("dve", mlab[h].v(), t_.v(), sz.v(), ALU.mult)

                q_prologue(0)
                for h in range(H):
                    if h + 1 < H:
                        q_prologue(h + 1)
                    attend(h)
                    epilogue(h)
                out_branch("w_mla_out", mlab, "gm", False)

                for k in range(KD):
                    P.copy("act" if k % 2 == 0 else "dve", hT[k].v(), merged[k].v())
                NCO = WELEMS // KD
                for tt_ in range(4):
                    r0 = m * SEG + tt_ * 128
                    xr = xts[tt_ % 2]
                    P.dma("sp", xr.v(), X(xown[r0:r0 + 128, :]))
                    for c0 in range(0, D, NCO):
                        ncw = min(NCO, D - c0)
                        wo_ = wcols("w_o", c0, need=ncw)
                        pb = ps()
                        for kt in range(KD):
                            P.mm(pb[:, 0:ncw], hT[kt][:, tt_ * 128:(tt_ + 1) * 128], wo_(kt), start=(kt == 0),
                                 stop=(kt == KD - 1))
                        P.tt("dve", xr[:, c0:c0 + ncw], pb[:, 0:ncw], xr[:, c0:c0 + ncw], ALU.add)
                    P.dma("sp", View(None, outd[r0:r0 + 128, :]), xr.v())
            P.wait_all_dma("sp")
            P.emit()
    return nc


def host_inputs(cfg, inp):
    d = dims(cfg)
    D, SEQ, B, CW, H, QL, KVL, MH, MEMT = (cfg[k] for k in ("D", "SEQ", "B", "CW", "H", "QL", "KVL", "MH", "MEMT"))
    KD, NST, NSEG, MW, NG = d["KD"], d["NST"], d["NSEG"], d["MW"], d["NG"]
    offs, gcol = d["offs"], d["gcol"]
    f = lambda a: np.ascontiguousarray(np.asarray(a, dtype=np.float32))
    x = f(inp["x"])
    pos = np.ascontiguousarray(np.asarray(inp["positions"], dtype=np.int32))
    mem = f(inp["mem"])
    w_in = f(inp["w_in"][0])
    KC = CW // 128

    def cols(v, n):
        return np.asarray(v, np.float32).reshape(n, 128).T

    gv = np.zeros((128, NG), np.float32)
    gv[:, gcol["norm"]:gcol["norm"] + KD] = cols(inp["norm_g"][0], KD)
    gv[:, gcol["memnorm"]:gcol["memnorm"] + KD] = cols(inp["mem_norm_g"][0], KD)
    gv[:, gcol["qn"]:gcol["qn"] + QL // 128] = cols(inp["mla_q_norm_g"][0], QL // 128)
    gv[:, gcol["kvn"]:gcol["kvn"] + KVL // 128] = cols(inp["mla_kv_norm_g"][0], KVL // 128)
    gv[:, gcol["qnn"]] = np.asarray(inp["mla_qn_nope_g"][0], np.float32)
    gv[:, gcol["knn"]] = np.asarray(inp["mla_kn_nope_g"][0], np.float32)
    qr = np.asarray(inp["mla_qn_rope_g"][0], np.float32)
    kr = np.asarray(inp["mla_kn_rope_g"][0], np.float32)
    sw = lambda v: np.concatenate([v[32:64], v[0:32]])
    gv[0:64, gcol["qnr"]] = qr
    gv[0:64, gcol["qnrsw"]] = sw(qr)
    gv[0:64, gcol["knr"]] = kr
    gv[0:64, gcol["knrsw"]] = sw(kr)
    gv[:, gcol["memq"]:gcol["memq"] + 2] = cols(inp["mem_qn_g"][0], 2)
    gv[:, gcol["memk"]:gcol["memk"] + 2] = cols(inp["mem_kn_g"][0], 2)
    cw = np.asarray(inp["conv_w"][0], np.float32)
    for j in range(3):
        gv[:, gcol["conv"] + j * KC:gcol["conv"] + (j + 1) * KC] = cols(cw[j], KC)
    invf = np.power(np.float32(10000.0), -np.arange(32, dtype=np.float32) / np.float32(32)).astype(np.float32)
    gv[0:64, gcol["invf"]] = np.concatenate([invf, invf])
    gv[0:64, gcol["sgn"]] = np.concatenate([-np.ones(32, np.float32), np.ones(32, np.float32)])

    blocks = []
    for j in range(KC):
        for nm in ("cg", "bg", "u", "cz"):
            blocks.append(w_in[:, offs[nm] + j * 128:offs[nm] + (j + 1) * 128])
    w_conv4 = np.ascontiguousarray(np.concatenate(blocks, axis=1))
    a = offs["kr"]
    w_krsw = np.ascontiguousarray(np.concatenate([w_in[:, a + 32:a + 64], w_in[:, a:a + 32]], axis=1))
    w_uq = f(inp["w_uq"][0])
    w_uqsw = np.ascontiguousarray(np.concatenate(
        [np.concatenate([w_uq[:, h * 192 + 160:h * 192 + 192], w_uq[:, h * 192 + 128:h * 192 + 160]], axis=1)
         for h in range(H)], axis=1))
    w_ukv = f(inp["w_ukv"][0])
    w_ukv_k = np.ascontiguousarray(np.concatenate([w_ukv[:, h * 256:h * 256 + 128] for h in range(H)], axis=1))
    w_ukv_v = np.ascontiguousarray(np.concatenate([w_ukv[:, h * 256 + 128:h * 256 + 256] for h in range(H)], axis=1))
    ident = np.eye(128, dtype=np.float32)
    kk = np.arange(SEG)[:, None] // 64
    qq = np.arange(SEG)[None, :] // 64
    diag = (kk <= qq).astype(np.float32)
    diagm = np.ascontiguousarray(diag.reshape(4, 128, SEG).transpose(1, 0, 2).reshape(128, 4 * SEG))
    common = dict(ident=ident, diagm=diagm, gv=gv, w_in=w_in, w_conv4=w_conv4, w_krsw=w_krsw, w_uq=w_uq, w_uqsw=w_uqsw,
                  w_ukv_k=w_ukv_k, w_ukv_v=w_ukv_v, w_conv_out=f(inp["w_conv_out"][0]),
                  w_mla_out=f(inp["w_mla_out"][0]), w_mem_kv=f(inp["w_mem_kv"][0]),
                  w_mem_out=f(inp["w_mem_out"][0]), w_o=f(inp["w_o"][0]))
    maps = []
    for core in range(4 * B):
        b, c = core // 4, core % 4
        segs = [4 * m + c for m in range(NSEG)]
        xown = np.concatenate([x[b, s * SEG:(s + 1) * SEG] for s in segs], axis=0)
        xhalo = np.zeros((2 * NSEG, D), np.float32)
        for m, s in enumerate(segs):
            if s > 0:
                xhalo[2 * m:2 * m + 2] = x[b, s * SEG - 2:s * SEG]
        posown = np.concatenate([pos[b, s * SEG:(s + 1) * SEG] for s in segs])[None, :]
        ab = np.zeros((128, 8), np.float32)
        for r in range(4):
            if r < c:
                ab[:, 2 * r + 1] = 1.0
            elif r == c:
                ab[:, 2 * r] = 1.0
        mp = dict(common)
        mp.update(xall=np.ascontiguousarray(x[b]), xown=np.ascontiguousarray(xown), xhalo=xhalo,
                  posall=np.ascontiguousarray(pos[b][None, :]), posown=np.ascontiguousarray(posown),
                  memb=np.ascontiguousarray(mem[b]), ab=ab)
        maps.append(mp)
    return maps


def assemble(cfg, results):
    d = dims(cfg)
    D, SEQ, B = cfg["D"], cfg["SEQ"], cfg["B"]
    NSEG = d["NSEG"]
    out = np.zeros((B, SEQ, D), np.float32)
    for core in range(4 * B):
        b, c = core // 4, core % 4
        o = results[core]["out"]
        for m in range(NSEG):
            s = 4 * m + c
            out[b, s * SEG:(s + 1) * SEG] = o[m * SEG:(m + 1) * SEG]
    return out


def kernel(**inputs):
    cfg = CFG
    nc = build_nc(cfg)
    maps = host_inputs(cfg, inputs)
    res = run_bass_kernel_spmd(nc, maps, core_ids=list(range(4 * cfg["B"])))
    return assemble(cfg, res.results)
```

```python
import numpy as np
from contextlib import ExitStack
import concourse.bass as bass
import concourse.mybir as mybir
from concourse.bass_utils import run_bass_kernel_spmd

F32 = mybir.dt.float32
BF16 = mybir.dt.bfloat16
I32 = mybir.dt.int32
AF = mybir.ActivationFunctionType
ALU = mybir.AluOpType

CFG = dict(D=2048, SEQ=8192, B=2, CW=1024, H=16, QL=512, KVL=512, MH=4, MEMT=256, debug=False)
EPS = 1e-6
SAME_SYNC = True
SEG = 512
PI = float(np.pi)


ALLBUFS = []


def reset_tracking():
    for b in ALLBUFS:
        b.w = {}
        b.r = {}
        b.sem = None
        b.dcount = 0


class Buf:
    def __init__(self, t, name, psum=False):
        self.t = t
        self.name = name
        self.psum = psum
        self.w = {}
        self.r = {}
        self.sem = None
        self.dcount = 0
        ALLBUFS.append(self)

    def __getitem__(self, k):
        return View(self, self.t[k])

    def v(self):
        return View(self, self.t[:])


class View:
    def __init__(self, buf, ap):
        self.buf = buf
        self.ap = ap


class Instr:
    __slots__ = ("fn", "waits", "dma")

    def __init__(self, fn, waits, dma=None):
        self.fn = fn
        self.waits = waits
        self.dma = dma


def _bufs(lst):
    out = []
    for x in lst:
        if x is None or isinstance(x, (int, float)):
            continue
        b = x.buf if isinstance(x, View) else x
        if b is not None and b not in out:
            out.append(b)
    return out


class Prog:
    ENGS = ["pe", "act", "dve", "pool", "sp"]

    NPROG = [0]

    def __init__(self, nc, stack):
        self.nc = nc
        self.stack = stack
        Prog.NPROG[0] += 1
        self.tag = "p%d" % Prog.NPROG[0]
        self.ins = {e: [] for e in self.ENGS}
        self.seen = {e: {} for e in self.ENGS}
        self.marked = {e: set() for e in self.ENGS}
        self.csem = {e: stack.enter_context(nc.semaphore(self.tag + "c_" + e)) for e in self.ENGS}
        self.dbufs = []
        self.psi = 0

    def _collect(self, eng, reads, writes, skip_dma_id=None):
        toks = []
        for b in reads:
            toks += list(b.w.values())
            if b.psum:
                toks += list(b.r.values())
        for b in writes:
            for tk in b.w.values():
                if skip_dma_id is not None and tk[0] == "d" and tk[1] == skip_dma_id:
                    continue
                toks.append(tk)
            toks += list(b.r.values())
        waits = []
        for tk in toks:
            if tk[0] == "c":
                e2, idx = tk[1], tk[2]
                if e2 == eng and (eng in ("pe", "sp") or not SAME_SYNC):
                    continue
                key, val = ("c", e2), idx
            else:
                key, val = ("d", tk[1]), tk[2]
            if self.seen[eng].get(key, -1) >= val:
                continue
            self.seen[eng][key] = val
            waits.append(tk)
            if tk[0] == "c":
                self.marked[tk[1]].add(tk[2])
        return waits

    def add(self, eng, fn, reads, writes):
        reads = _bufs(reads)
        writes = _bufs(writes)
        waits = self._collect(eng, reads, writes)
        idx = len(self.ins[eng])
        tok = ("c", eng, idx)
        for b in reads:
            b.r[eng] = tok
        for b in writes:
            b.w = {eng: tok}
            b.r = {}
        self.ins[eng].append(Instr(fn, waits))

    def dma(self, q, out, in_, sem_buf=None):
        ob, ib = out.buf, in_.buf
        sb = sem_buf or (ob if (ob is not None and not getattr(ob, "dram", False)) else ib)
        if sb is None:
            sb = ob
        if sb.sem is None:
            sb.sem = self.stack.enter_context(self.nc.semaphore(self.tag + "d_" + sb.name))
            self.dbufs.append(sb)
        reads = _bufs([ib])
        writes = _bufs([ob])
        waits = self._collect(q, reads, writes, skip_dma_id=id(sb))
        sb.dcount += 1
        tok = ("d", id(sb), sb.dcount, sb)
        key = ("d", id(sb))
        for b in reads:
            b.r[key] = tok
        for b in writes:
            b.w = {key: tok}
            b.r = {}
        oap, iap = out.ap, in_.ap
        self.ins[q].append(Instr(lambda e: e.dma_start(out=oap, in_=iap), waits, dma=sb))

    def wait_all_dma(self, eng="sp"):
        waits = []
        for b in self.dbufs:
            if b.dcount > 0:
                key = ("d", id(b))
                if self.seen[eng].get(key, -1) >= b.dcount:
                    continue
                self.seen[eng][key] = b.dcount
                waits.append(("d", id(b), b.dcount, b))
        self.ins[eng].append(Instr(lambda e: e.nop(), waits))

    def emit(self):
        rank = {}
        for e in self.ENGS:
            cnt = 0
            for i in range(len(self.ins[e])):
                if i in self.marked[e]:
                    cnt += 1
                    rank[(e, i)] = cnt
        csem = self.csem
        marked = self.marked

        def body(name):
            lst = self.ins[name]

            def f(e):
                for i, ins in enumerate(lst):
                    for tk in ins.waits:
                        if tk[0] == "c":
                            e.wait_ge(csem[tk[1]], rank[(tk[1], tk[2])])
                        else:
                            e.wait_ge(tk[3].sem, 16 * tk[2])
                    r = ins.fn(e)
                    if ins.dma is not None:
                        r.then_inc(ins.dma.sem, 16)
                    elif i in marked[name]:
                        r.then_inc(csem[name], 1)
            return f

        with self.nc.Block() as block:
            block.tensor(body("pe"))
            block.scalar(body("act"))
            block.vector(body("dve"))
            block.gpsimd(body("pool"))
            block.sync(body("sp"))

    def mm(self, out, lhsT, rhs, start=True, stop=True):
        o, l, r = out.ap, lhsT.ap, rhs.ap
        if start and out.buf.psum:
            assert not (out.buf.w and not out.buf.r), "PSUM bank %s reused before being read" % out.buf.name
        self.add("pe", lambda e: e.matmul(o, l, r, start=start, stop=stop), [lhsT, rhs], [out])

    def tr(self, out, in_, ident):
        o, i, d = out.ap, in_.ap, ident.ap
        self.add("pe", lambda e: e.transpose(o, i, d), [in_, ident], [out])

    def act(self, out, in_, func, scale=1.0, bias=0.0, accum=None):
        o, i = out.ap, in_.ap
        sc = scale.ap if isinstance(scale, View) else scale
        bi = bias.ap if isinstance(bias, View) else bias
        ac = accum.ap if accum is not None else None
        rd = [in_, scale if isinstance(scale, View) else None, bias if isinstance(bias, View) else None]
        self.add("act", lambda e: e.activation(out=o, in_=i, func=func, bias=bi, scale=sc, accum_out=ac),
                 rd, [out, accum])

    def ts(self, eng, out, in0, s1, s2, op0, op1=None):
        o, i = out.ap, in0.ap
        a1 = s1.ap if isinstance(s1, View) else s1
        a2 = s2.ap if isinstance(s2, View) else s2
        rd = [in0, s1 if isinstance(s1, View) else None, s2 if isinstance(s2, View) else None]
        if op1 is None:
            self.add(eng, lambda e: e.tensor_scalar(out=o, in0=i, scalar1=a1, scalar2=None, op0=op0), rd, [out])
        else:
            self.add(eng, lambda e: e.tensor_scalar(out=o, in0=i, scalar1=a1, scalar2=a2, op0=op0, op1=op1),
                     rd, [out])

    def tt(self, eng, out, in0, in1, op):
        o, a, b = out.ap, in0.ap, in1.ap
        self.add(eng, lambda e: e.tensor_tensor(out=o, in0=a, in1=b, op=op), [in0, in1], [out])

    def stt(self, out, in0, scalar, in1, op0, op1, accum=None):
        o, a, b = out.ap, in0.ap, in1.ap
        s = scalar.ap if isinstance(scalar, View) else scalar
        ac = accum.ap if accum is not None else None
        self.add("dve", lambda e: e.scalar_tensor_tensor(out=o, in0=a, scalar=s, in1=b, op0=op0, op1=op1,
                                                         accum_out=ac),
                 [in0, in1, scalar if isinstance(scalar, View) else None], [out, accum])

    def copy(self, eng, out, in_):
        if eng == "act":
            self.act(out, in_, AF.Copy)
        else:
            o, i = out.ap, in_.ap
            self.add(eng, lambda e: e.tensor_copy(out=o, in_=i), [in_], [out])

    def recip(self, out, in_):
        o, i = out.ap, in_.ap
        self.add("dve", lambda e: e.reciprocal(out=o, in_=i), [in_], [out])

    def memset(self, eng, v, val):
        a = v.ap
        self.add(eng, lambda e: e.memset(a, val), [], [v])


def dims(cfg):
    d = dict(cfg)
    d["KD"] = cfg["D"] // 128
    d["NST"] = cfg["SEQ"] // SEG
    d["NSEG"] = d["NST"] // 4
    d["MW"] = cfg["MH"] * 256
    d["MLAW"] = cfg["H"] * 128
    D, CW, QL, KVL, MW, MLAW = cfg["D"], cfg["CW"], cfg["QL"], cfg["KVL"], d["MW"], d["MLAW"]
    offs = {}
    o = 0
    for nm, w in [("cg", CW), ("bg", CW), ("u", CW), ("cz", CW), ("cq", QL), ("ckv", KVL), ("kr", 64),
                  ("mz", MLAW), ("mq", MW), ("mmz", MW), ("gc", D), ("gm", D), ("gmem", D)]:
        offs[nm] = o
        o += w
    d["offs"] = offs
    d["INW"] = o
    g = {}
    c = 0
    for nm, n in [("norm", d["KD"]), ("memnorm", d["KD"]), ("qn", QL // 128), ("kvn", KVL // 128),
                  ("qnn", 1), ("knn", 1), ("qnr", 1), ("qnrsw", 1), ("knr", 1), ("knrsw", 1),
                  ("memq", 2), ("memk", 2), ("conv", 3 * (CW // 128)), ("invf", 1), ("sgn", 1)]:
        g[nm] = c
        c += n
    d["gcol"] = g
    d["NG"] = c
    return d


def build_nc(cfg):
    del ALLBUFS[:]
    d = dims(cfg)
    D, SEQ, CW, H, QL, KVL, MH, MEMT = (cfg[k] for k in ("D", "SEQ", "CW", "H", "QL", "KVL", "MH", "MEMT"))
    KD, NST, NSEG, MW, MLAW, INW, NG = (d[k] for k in ("KD", "NST", "NSEG", "MW", "MLAW", "INW", "NG"))
    offs, gcol = d["offs"], d["gcol"]
    KQ, KKV, KC, KM = QL // 128, KVL // 128, CW // 128, MW // 128
    NOWN = NSEG * SEG
    WELEMS = 4096

    nc = bass.Bass("TRN2", target_bir_lowering=False)

    def din(name, shape, dt=F32):
        return nc.dram_tensor(name, list(shape), dt, kind="ExternalInput").ap()

    xall = din("xall", [SEQ, D])
    xown = din("xown", [NOWN, D])
    xhalo = din("xhalo", [2 * NSEG, D])
    posall = din("posall", [1, SEQ], I32)
    posown = din("posown", [1, NOWN], I32)
    memb = din("memb", [MEMT, D])
    diagd = din("diagm", [128, 4 * SEG])
    abd = din("ab", [128, 8])
    identd = din("ident", [128, 128])
    gvd = din("gv", [128, NG])
    w_in = din("w_in", [D, INW])
    w_conv4 = din("w_conv4", [D, 4 * CW])
    w_krsw = din("w_krsw", [D, 64])
    w_uq = din("w_uq", [QL, H * 192])
    w_uqsw = din("w_uqsw", [QL, H * 64])
    w_ukv_k = din("w_ukv_k", [KVL, H * 128])
    w_ukv_v = din("w_ukv_v", [KVL, H * 128])
    w_conv_out = din("w_conv_out", [CW, D])
    w_mla_out = din("w_mla_out", [MLAW, D])
    w_mem_kv = din("w_mem_kv", [D, 2 * MW])
    w_mem_out = din("w_mem_out", [MW, D])
    w_o = din("w_o", [D, D])
    outd = nc.dram_tensor("out", [NOWN, D], F32, kind="ExternalOutput").ap()
    K_all = nc.dram_tensor("K_all", [128, H, SEQ], BF16, kind="Internal").ap()
    V_all = nc.dram_tensor("V_all", [128, H, SEQ // 128, 128], BF16, kind="Internal").ap()
    dbg_outs = {}
    streams = {}

    def def_stream(name, wap, nk, c0, width, ncol=None):
        ncol = ncol or (WELEMS // nk)
        nch = (width + ncol - 1) // ncol
        scr = nc.dram_tensor("wsc_" + name, [nch, 128, nk * ncol], BF16, kind="Internal").ap()
        streams[name] = dict(wap=wap, nk=nk, c0=c0, width=width, ncol=ncol, nch=nch, scr=scr)

    def_stream("conv4", w_conv4, KD, 0, 4 * CW)
    def_stream("cq", w_in, KD, offs["cq"], QL)
    def_stream("mz", w_in, KD, offs["mz"], MLAW)
    def_stream("mq", w_in, KD, offs["mq"], MW)
    def_stream("mmz", w_in, KD, offs["mmz"], MW)
    def_stream("gc", w_in, KD, offs["gc"], D)
    def_stream("gm", w_in, KD, offs["gm"], D)
    def_stream("gmem", w_in, KD, offs["gmem"], D)
    def_stream("memkv", w_mem_kv, KD, 0, 2 * MW)
    def_stream("w_conv_out", w_conv_out, KC, 0, D)
    def_stream("w_mem_out", w_mem_out, KM, 0, D)
    def_stream("w_mla_out", w_mla_out, H, 0, D)
    KDH = max(1, KD // 2)
    streams_r0 = {}
    def_stream("w_o_a", w_o, KDH, 0, D, ncol=min(512, D, WELEMS // KDH))
    streams_r0["w_o_a"] = 0
    if KD > 1:
        def_stream("w_o_b", w_o, KD - KDH, 0, D, ncol=min(512, D, WELEMS // KDH))
        streams_r0["w_o_b"] = KDH * 128
    def_stream("w_uq", w_uq, KQ, 0, H * 192, ncol=192 * max(1, (WELEMS // KQ) // 192))
    def_stream("w_uqsw", w_uqsw, KQ, 0, H * 64)

    def X(ap):
        return View(None, ap)

    def wview(wap, r0, nk, c0, ncol):
        return X(wap[r0:r0 + nk * 128, c0:c0 + ncol].rearrange("(kt p) c -> p kt c", p=128))

    with ExitStack() as gs:
        uniq = {"n": 0}

        def sb(name, shape, dt=F32, st=gs):
            uniq["n"] += 1
            nm = "s%d_%s" % (uniq["n"], name)
            return Buf(st.enter_context(nc.sbuf_tensor(nm, list(shape), dt)), nm)

        ident = sb("ident", [128, 128])
        gv = sb("gv", [128, NG])
        cm = {}
        for n in (1, 64, 128, 256, QL, KVL):
            if n not in cm:
                cm[n] = sb("cm%d" % n, [128, 128], BF16)
        kropeT = sb("kropeT", [64, SEQ], BF16)
        banks = [Buf(gs.enter_context(nc.psum_tensor("ps%d" % i, [128, 512], F32)), "ps%d" % i, psum=True)
                 for i in range(8)]
        Kd = []
        Vd = []
        for st_ in range(NST):
            kb = Buf(K_all[:, :, st_ * SEG:(st_ + 1) * SEG], "Kd%d" % st_)
            kb.dram = True
            vb = Buf(V_all[:, :, st_ * 4:(st_ + 1) * 4, :], "Vd%d" % st_)
            vb.dram = True
            Kd.append(kb)
            Vd.append(vb)

        def G(name, j=0, rows=128):
            c = gcol[name] + j
            return gv[0:rows, c:c + 1]

        state = {"psi": 0}

        def make_ps(P):
            def ps():
                b = banks[state["psi"] % 4]
                state["psi"] += 1
                return b
            return ps

        def nt_x(P, src_ap, nrows, xt, junk, ss):
            P.dma("sp", xt[0:nrows, :], X(src_ap))
            P.act(junk[0:nrows, :], xt[0:nrows, :], AF.Square, accum=ss[0:nrows, 0:1])
            P.act(ss[0:nrows, 1:2], ss[0:nrows, 0:1], AF.Ln, scale=1.0 / D, bias=epsc[0:nrows, 0:1])
            P.act(ss[0:nrows, 2:3], ss[0:nrows, 1:2], AF.Exp, scale=-0.5)
            P.ts("dve", xt[0:nrows, :], xt[0:nrows, :], ss[0:nrows, 2:3], None, ALU.mult)

        def nt_y(P, ps, nrows, xt, hT_dst, gname, col0):
            for k0 in range(0, KD, 4):
                pb = ps()
                nk = min(4, KD - k0)
                for j in range(nk):
                    kt = k0 + j
                    P.tr(pb[:, j * 128:j * 128 + nrows], xt[0:nrows, kt * 128:(kt + 1) * 128],
                         ident[0:nrows, 0:nrows])
                for j in range(nk):
                    kt = k0 + j
                    dst = hT_dst[kt][:, col0:col0 + nrows]
                    src = pb[:, j * 128:j * 128 + nrows]
                    if kt % 2 == 0:
                        P.act(dst, src, AF.Copy, scale=G(gname, kt))
                    else:
                        P.ts("dve", dst, src, G(gname, kt), None, ALU.mult)

        def norm_transpose(P, ps, src_ap, nrows, xt, junk, ss, hT_dst, gname, col0):
            nt_x(P, src_ap, nrows, xt, junk, ss)
            nt_y(P, ps, nrows, xt, hT_dst, gname, col0)

        def rope_tables(P, pos_ap, n, posi, ang, cosT, sinT, tmr):
            P.dma("sp", posi[:, 0:n], X(pos_ap.partition_broadcast(64)))
            P.copy("dve", ang[:, 0:n], posi[:, 0:n])
            P.ts("dve", ang[:, 0:n], ang[:, 0:n], G("invf", 0, 64), None, ALU.mult)
            for dst, off in ((cosT, 0.75), (sinT, 0.5)):
                P.ts("dve", dst[:, 0:n], ang[:, 0:n], 1.0 / (2 * PI), off, ALU.mult, ALU.add)
                P.copy("dve", posi[:, 0:n], dst[:, 0:n])
                P.copy("dve", tmr[:, 0:n], posi[:, 0:n])
                P.tt("dve", dst[:, 0:n], dst[:, 0:n], tmr[:, 0:n], ALU.subtract)
                P.stt(dst[:, 0:n], dst[:, 0:n], 0.0, dst[:, 0:n], ALU.is_lt, ALU.add)
                P.act(dst[:, 0:n], dst[:, 0:n], AF.Sin, scale=2 * PI, bias=negpi[0:64, 0:1])
            P.ts("dve", sinT[:, 0:n], sinT[:, 0:n], G("sgn", 0, 64), None, ALU.mult)

        def rms_feat(P, ps, srcs, n, cmat, rs, rows=128, sqb=None):
            pst = ps()
            for i, s in enumerate(srcs):
                q = sqb[i % len(sqb)]
                P.act(q[0:rows, 0:n], s, AF.Square)
                P.mm(pst[0:rows, 0:n], cmat[0:rows, 0:rows], q[0:rows, 0:n], start=(i == 0),
                     stop=(i == len(srcs) - 1))
            P.act(rs[0:rows, 0:n], pst[0:rows, 0:n], AF.Ln, bias=epsc[0:rows, 0:1])
            P.act(rs[0:rows, 0:n], rs[0:rows, 0:n], AF.Exp, scale=-0.5)

        negpi = sb("negpi", [128, 1])
        epsc = sb("epsc", [128, 1])

        with ExitStack() as s1:
            P = Prog(nc, gs)
            ps = make_ps(P)

            def t1(name, shape, dt=F32):
                return sb(name, shape, dt, st=s1)

            P.dma("sp", ident.v(), X(identd))
            P.dma("sp", gv.v(), X(gvd))
            for n, t in cm.items():
                P.memset("dve", t.v(), 1.0 / n)
            P.memset("dve", negpi.v(), -PI)
            P.memset("dve", epsc.v(), EPS)

            wkv = t1("wkv", [128, KD, KVL + 128], BF16)
            wk = t1("wk", [128, KKV, H * 128], BF16)
            wv = t1("wv", [128, KKV, H * 128], BF16)
            P.dma("pool", wkv[:, :, 0:KVL + 64], wview(w_in, 0, KD, offs["ckv"], KVL + 64))
            P.dma("pool", wkv[:, :, KVL + 64:KVL + 128], wview(w_krsw, 0, KD, 0, 64))
            P.dma("pool", wk.v(), wview(w_ukv_k, 0, KKV, 0, H * 128))
            P.dma("pool", wv.v(), wview(w_ukv_v, 0, KKV, 0, H * 128))
            wconv_sem = Buf(None, "wconv")
            for nm_, sd in streams.items():
                for j in range(sd["nch"]):
                    cw_ = min(sd["ncol"], sd["width"] - j * sd["ncol"])
                    dst = sd["scr"][j].rearrange("p (k c) -> p k c", c=sd["ncol"])[:, :, 0:cw_]
                    P.dma("pool", View(None, dst),
                          wview(sd["wap"], streams_r0.get(nm_, 0), sd["nk"], sd["c0"] + j * sd["ncol"], cw_),
                          sem_buf=wconv_sem)

            xts = [t1("xt%d" % i, [128, D]) for i in range(2)]
            xnb = [t1("xnb%d" % i, [128, D], BF16) for i in range(2)]
            sss = [t1("ss%d" % i, [128, 4]) for i in range(2)]
            GK = min(4, KD)
            NGK = KD // GK
            hTg = [[t1("hT%d_%d" % (s_, g), [128, GK, SEG], BF16) for g in range(NGK)] for s_ in range(2)]
            identb = t1("identb", [128, 128], BF16)
            P.copy("dve", identb.v(), ident.v())

            def hTv(set_, kt):
                return hTg[set_][kt // GK][:, kt % GK, :]

            ckvf = [t1("ckvf%d" % k, [128, SEG]) for k in range(KKV)]
            ckvn = [t1("ckvn%d" % k, [128, SEG], BF16) for k in range(KKV)]
            sqb = [t1("sqb%d" % i, [128, SEG], BF16) for i in range(2)]
            rsb = [t1("rs%d" % i, [128, SEG]) for i in range(2)]
            rsc = [t1("rsc%d" % i, [128, SEG]) for i in range(2)]
            Kst = t1("Kst", [128, H, SEG], BF16)
            Vst = t1("Vst", [128, H, 4, 128], BF16)
            posi = t1("posi", [64, SEG], I32)
            cosT = t1("cosT", [64, SEG])
            sinT = t1("sinT", [64, SEG])
            tm1 = t1("tm1", [64, SEG])
            tm2 = t1("tm2", [64, SEG])
            ang = tm1

            for kt in range(KD):
                P.ts("dve", wkv[:, kt, :], wkv[:, kt, :], G("norm", kt), None, ALU.mult)

            NTILE = 4 * NST

            def xstage(n):
                if n >= NTILE:
                    return
                i2 = n % 2
                xt, xb, ss = xts[i2], xnb[i2], sss[i2]
                P.dma("sp", xt.v(), X(xall[n * 128:(n + 1) * 128, :]))
                P.stt(xb.v(), xt.v(), 1.0, xt.v(), ALU.mult, ALU.mult, accum=ss[:, 0:1])
                P.act(ss[:, 1:2], ss[:, 0:1], AF.Ln, scale=1.0 / D, bias=epsc[:, 0:1])
                P.act(ss[:, 2:3], ss[:, 1:2], AF.Exp, scale=-0.5)
                P.ts("dve", xb.v(), xt.v(), ss[:, 2:3], None, ALU.mult)

            def ystage(n):
                if n >= NTILE:
                    return
                st_, tt_ = n // 4, n % 4
                xb = xnb[n % 2]
                for g in range(NGK):
                    pb = ps()
                    pbh = pb.t.bitcast(BF16)
                    for j in range(GK):
                        kt = g * GK + j
                        P.tr(View(pb, pbh[:, j * 128:(j + 1) * 128]), xb[:, kt * 128:(kt + 1) * 128], identb.v())
                    src = View(pb, pbh[:, 0:GK * 128].rearrange("p (k c) -> p k c", c=128))
                    P.copy("act" if g % 2 == 0 else "dve", hTg[st_ % 2][g][:, :, tt_ * 128:(tt_ + 1) * 128], src)
                xstage(n + 2)

            xstage(0)
            xstage(1)
            for n in range(4):
                ystage(n)

            def vstage(tt_):
                for h0 in range(0, H, 4):
                    nh = min(4, H - h0)
                    pb = ps()
                    for kt in range(KKV):
                        P.mm(pb[:, 0:nh * 128], ckvn[kt][:, tt_ * 128:(tt_ + 1) * 128],
                             wv[:, kt, h0 * 128:(h0 + nh) * 128], start=(kt == 0), stop=(kt == KKV - 1))
                    P.copy("dve", Vst[:, h0:h0 + nh, tt_, :],
                           View(pb, pb.t[:, 0:nh * 128].rearrange("p (h d) -> p h d", d=128)))

            ins_after = [min(H + 2, (q + 1) * max(1, H // 4) - 1) for q in range(4)]
            for st_ in range(NST):
                set_ = st_ % 2
                rope_tables(P, posall[0:1, st_ * SEG:(st_ + 1) * SEG], SEG, posi, ang, cosT, sinT, tm2)
                for blk in range(KKV):
                    pb = ps()
                    for kt in range(KD):
                        P.mm(pb.v(), wkv[:, kt, blk * 128:(blk + 1) * 128], hTv(set_, kt), start=(kt == 0),
                             stop=(kt == KD - 1))
                    P.copy("act", ckvf[blk].v(), pb.v())
                rs = rsc[0]
                rms_feat(P, ps, [c.v() for c in ckvf], SEG, cm[KVL], rs, sqb=sqb)
                for blk in range(KKV):
                    P.stt(ckvn[blk].v(), ckvf[blk].v(), G("kvn", blk), rs.v(), ALU.mult, ALU.mult)
                pkr = ps()
                for kt in range(KD):
                    P.mm(pkr[0:64, :], wkv[:, kt, KVL:KVL + 64], hTv(set_, kt), start=(kt == 0), stop=(kt == KD - 1))
                pks = ps()
                for kt in range(KD):
                    P.mm(pks[0:64, :], wkv[:, kt, KVL + 64:KVL + 128], hTv(set_, kt), start=(kt == 0),
                         stop=(kt == KD - 1))
                rs = rsc[1]
                rms_feat(P, ps, [pkr[0:64, :]], SEG, cm[64], rs, rows=64, sqb=sqb)
                P.stt(tm1.v(), pkr[0:64, :], G("knr", 0, 64), cosT.v(), ALU.mult, ALU.mult)
                P.stt(tm2.v(), pks[0:64, :], G("knrsw", 0, 64), sinT.v(), ALU.mult, ALU.mult)
                P.tt("dve", tm1.v(), tm1.v(), tm2.v(), ALU.add)
                P.tt("dve", kropeT[:, st_ * SEG:(st_ + 1) * SEG], tm1.v(), rs[0:64, :], ALU.mult)
                pA = {}
                pB = {}
                for t in range(H + 3):
                    if t < H:
                        pA[t] = banks[4 + (t % 4)]
                        for kt in range(KKV):
                            P.mm(pA[t].v(), wk[:, kt, t * 128:(t + 1) * 128], ckvn[kt].v(), start=(kt == 0),
                                 stop=(kt == KKV - 1))
                        P.act(sqb[t % 2].v(), pA[t].v(), AF.Square)
                    h1 = t - 1
                    if 0 <= h1 < H:
                        pB[h1] = ps()
                        P.mm(pB[h1].v(), cm[128].v(), sqb[h1 % 2].v())
                        P.act(rsb[h1 % 2].v(), pB[h1].v(), AF.Ln, bias=epsc[:, 0:1])
                        P.act(rsb[h1 % 2].v(), rsb[h1 % 2].v(), AF.Exp, scale=-0.5)
                    h3 = t - 2
                    if 0 <= h3 < H:
                        P.stt(Kst[:, h3, :], pA[h3].v(), G("knn"), rsb[h3 % 2].v(), ALU.mult, ALU.mult)
                    for q in range(4):
                        if ins_after[q] == t:
                            vstage(q)
                            ystage(4 * (st_ + 1) + q)
                P.dma("sp", Kd[st_].v(), Kst.v())
                P.dma("sp", Vd[st_].v(), Vst.v())
            P.wait_all_dma("sp")
            P.emit()

        reset_tracking()
        with ExitStack() as s2:
            P = Prog(nc, gs)
            ps = make_ps(P)

            def t2(name, shape, dt=F32):
                return sb(name, shape, dt, st=s2)

            accb = banks[4:8]
            diagm = t2("diagm", [128, 4 * SEG], BF16)
            abt = t2("abt", [128, 8])
            tmpm = t2("tmpm", [128, SEG], BF16)
            P.dma("pool", diagm.v(), X(diagd))
            P.dma("sp", abt.v(), X(abd))
            wbufs = [t2("wb%d" % i, [128, WELEMS], BF16) for i in range(3)]
            wmeta = [None, None, None]
            wlru = [0, 0, 0]
            wstate = {"t": 1}

            def wcols(name, c0, need=128):
                sd = streams[name]
                nk, ncol = sd["nk"], sd["ncol"]
                j = c0 // ncol
                assert (c0 + need - 1) // ncol == j
                hit = None
                for i in range(3):
                    if wmeta[i] == (name, j):
                        hit = i
                if hit is None:
                    hit = min(range(3), key=lambda i: wlru[i])
                    wb = wbufs[hit]
                    cw_ = min(ncol, sd["width"] - j * ncol)
                    if cw_ == ncol:
                        P.dma("pool", View(wb, wb.t[:, 0:nk * ncol]), View(None, sd["scr"][j]))
                    else:
                        P.dma("pool", View(wb, wb.t[:, 0:nk * ncol].rearrange("p (k c) -> p k c", c=ncol)[:, :, 0:cw_]),
                              View(None, sd["scr"][j].rearrange("p (k c) -> p k c", c=ncol)[:, :, 0:cw_]))
                    wmeta[hit] = (name, j)
                wlru[hit] = wstate["t"]
                wstate["t"] += 1
                wb = wbufs[hit]
                off = c0 - j * ncol

                def get(kt, a=0, b=need):
                    return View(wb, wb.t[:, kt * ncol + off + a:kt * ncol + off + b])
                return get

            xts = [t2("xt%d" % i, [128, D]) for i in range(2)]
            junk = t2("junk", [128, D], BF16)
            sss = [t2("ss%d" % i, [128, 4]) for i in range(2)]
            hT = [t2("hT%d" % k, [128, SEG], BF16) for k in range(KD)]
            hTh = [t2("hTh%d" % k, [128, 2 * NSEG], BF16) for k in range(KD)]
            memT = hT
            Kmem = [[t2("Kmem%d_%d" % (h, dt_), [128, MEMT], BF16) for dt_ in range(2)] for h in range(MH)]
            Vmem = [t2("Vmem%d" % mt, [128, MW], BF16) for mt in range(MEMT // 128)]
            sqb = [t2("sqb%d" % i, [128, SEG], BF16) for i in range(2)]
            f32t = [t2("f%d" % i, [128, SEG]) for i in range(6)]
            fstate = {"i": 0}

            def ftmp():
                b = f32t[fstate["i"] % len(f32t)]
                fstate["i"] += 1
                return b

            cu = t2("cu", [128, SEG + 2])
            merged = [t2("mg%d" % k, [128, SEG]) for k in range(KD)]
            actb = [t2("ab%d" % k, [128, SEG], BF16) for k in range(max(KC, KM))]
            mlab = [t2("ml%d" % k, [128, SEG], BF16) for k in range(H)]
            cqn = [t2("cqn%d" % k, [128, SEG], BF16) for k in range(KQ)]
            Qn = [t2("Qn%d" % i, [128, SEG], BF16) for i in range(2)]
            Qr = [t2("Qr%d" % i, [64, SEG], BF16) for i in range(2)]
            Kc = [t2("Kc%d" % i, [128, 2 * SEG], BF16) for i in range(3)]
            Vc = [t2("Vc%d" % i, [128, 8, 128], BF16) for i in range(3)]
            kvstate = {"i": 0}
            Pt = [t2("Pt%d" % i, [128, SEG], BF16) for i in range(4)]
            posi = t2("posi", [64, SEG], I32)
            cosT = t2("cosT", [64, SEG])
            sinT = t2("sinT", [64, SEG])
            tm1 = t2("tm1", [64, SEG])
            tm2 = t2("tm2", [64, SEG])
            ang = tm1
            mqn = [t2("mqn%d" % i, [128, SEG], BF16) for i in range(2)]

            norm_transpose(P, ps, xhalo, 2 * NSEG, xts[0], junk, sss[0], hTh, "norm", 0)

            for mt in range(MEMT // 128):
                norm_transpose(P, ps, memb[mt * 128:(mt + 1) * 128, :], 128, xts[mt % 2], junk, sss[mt % 2], memT,
                               "memnorm", mt * 128)
            for h in range(MH):
                kfs = []
                for dt_ in range(2):
                    c0 = h * 256 + dt_ * 128
                    wg_ = wcols("memkv", c0)
                    pb = ps()
                    for kt in range(KD):
                        P.mm(pb[:, 0:MEMT], wg_(kt), memT[kt][:, 0:MEMT], start=(kt == 0), stop=(kt == KD - 1))
                    fb = ftmp()
                    P.copy("act", fb[:, 0:MEMT], pb[:, 0:MEMT])
                    kfs.append(fb)
                rs = ftmp()
                rms_feat(P, ps, [k[:, 0:MEMT] for k in kfs], MEMT, cm[256], rs, sqb=sqb)
                for dt_ in range(2):
                    P.stt(Kmem[h][dt_].v(), kfs[dt_][:, 0:MEMT], G("memk", dt_), rs[:, 0:MEMT], ALU.mult, ALU.mult)
            NCW = WELEMS // KD
            for c0 in range(0, MW, NCW):
                ncw = min(NCW, MW - c0)
                wg_ = wcols("memkv", MW + c0, need=ncw)
                for mt in range(MEMT // 128):
                    pb = ps()
                    for kt in range(KD):
                        P.mm(pb[:, 0:ncw], memT[kt][:, mt * 128:(mt + 1) * 128], wg_(kt), start=(kt == 0),
                             stop=(kt == KD - 1))
                    P.copy("act", Vmem[mt][:, c0:c0 + ncw], pb[:, 0:ncw])

            def proj_fm(name, c0, rhs_list, n=SEG, halo=None):
                wg_ = wcols(name, c0)
                nk = streams[name]["nk"]
                pb = ps()
                ph = ps() if halo is not None else None
                for kt in range(nk):
                    P.mm(pb[:, 0:n], wg_(kt), rhs_list[kt][:, 0:n] if isinstance(rhs_list[kt], Buf) else rhs_list[kt],
                         start=(kt == 0), stop=(kt == nk - 1))
                    if halo is not None:
                        P.mm(ph[:, 0:2], wg_(kt), halo[kt], start=(kt == 0), stop=(kt == nk - 1))
                return pb, ph

            def out_branch(name, act_list, gname, first):
                for ob in range(KD):
                    pb, _ = proj_fm(name, ob * 128, act_list)
                    pg, _ = proj_fm(gname, ob * 128, hT)
                    sg = ftmp()
                    P.act(sg.v(), pg.v(), AF.Sigmoid)
                    if first:
                        P.tt("dve", merged[ob].v(), pb.v(), sg.v(), ALU.mult)
                    else:
                        tmp = ftmp()
                        P.tt("dve", tmp.v(), pb.v(), sg.v(), ALU.mult)
                        P.tt("dve", merged[ob].v(), merged[ob].v(), tmp.v(), ALU.add)

            scale_mla = float((128 + 64) ** -0.5)
            scale_mem = float(256 ** -0.5)

            for m in range(NSEG):
                def x2(tt_):
                    r0 = m * SEG + tt_ * 128
                    nt_x(P, xown[r0:r0 + 128, :], 128, xts[tt_ % 2], junk, sss[tt_ % 2])

                x2(0)
                x2(1)
                for tt_ in range(4):
                    nt_y(P, ps, 128, xts[tt_ % 2], hT, "norm", tt_ * 128)
                    if tt_ + 2 < 4:
                        x2(tt_ + 2)
                rope_tables(P, posown[0:1, m * SEG:(m + 1) * SEG], SEG, posi, ang, cosT, sinT, tm2)
                hl = [hTh[kt][:, 2 * m:2 * m + 2] for kt in range(KD)]

                for j in range(KC):
                    pcg, phc = proj_fm("conv4", j * 512 + 0, hT, halo=hl)
                    fcg = ftmp()
                    P.copy("act", fcg.v(), pcg.v())
                    fh = ftmp()
                    P.copy("act", fh[:, 0:2], phc[:, 0:2])
                    pu, phu = proj_fm("conv4", j * 512 + 256, hT, halo=hl)
                    P.tt("dve", cu[:, 2:SEG + 2], fcg.v(), pu.v(), ALU.mult)
                    P.tt("dve", cu[:, 0:2], fh[:, 0:2], phu[:, 0:2], ALU.mult)
                    y = ftmp()
                    P.ts("dve", y.v(), cu[:, 0:SEG], G("conv", 0 * KC + j), None, ALU.mult)
                    P.stt(y.v(), cu[:, 1:SEG + 1], G("conv", 1 * KC + j), y.v(), ALU.mult, ALU.add)
                    P.stt(y.v(), cu[:, 2:SEG + 2], G("conv", 2 * KC + j), y.v(), ALU.mult, ALU.add)
                    pbg, _ = proj_fm("conv4", j * 512 + 128, hT)
                    P.tt("dve", y.v(), y.v(), pbg.v(), ALU.mult)
                    pz, _ = proj_fm("conv4", j * 512 + 384, hT)
                    sz = ftmp()
                    P.act(sz.v(), pz.v(), AF.Silu)
                    P.tt("dve", actb[j].v(), y.v(), sz.v(), ALU.mult)
                out_branch("w_conv_out", actb[0:KC], "gc", True)

                for h in range(MH):
                    qfs = []
                    for dt_ in range(2):
                        pq, _ = proj_fm("mq", h * 256 + dt_ * 128, hT)
                        fb = ftmp()
                        P.copy("act", fb.v(), pq.v())
                        qfs.append(fb)
                    rs = ftmp()
                    rms_feat(P, ps, [q.v() for q in qfs], SEG, cm[256], rs, sqb=sqb)
                    for dt_ in range(2):
                        P.stt(mqn[dt_].v(), qfs[dt_].v(), G("memq", dt_), rs.v(), ALU.mult, ALU.mult)
                    pts = []
                    for mt in range(MEMT // 128):
                        pb = ps()
                        for dt_ in range(2):
                            P.mm(pb.v(), Kmem[h][dt_][:, mt * 128:(mt + 1) * 128], mqn[dt_].v(), start=(dt_ == 0),
                                 stop=(dt_ == 1))
                        pt = Pt[mt % 4]
                        P.act(pt.v(), pb.v(), AF.Exp, scale=scale_mem)
                        pts.append(pt)
                    nmt = MEMT // 128
                    for dvt in range(2):
                        for mt in range(nmt):
                            P.mm(accb[dvt].v(), Vmem[mt][:, h * 256 + dvt * 128:h * 256 + (dvt + 1) * 128], pts[mt].v(),
                                 start=(mt == 0), stop=(mt == nmt - 1))
                    for mt in range(nmt):
                        P.mm(accb[2].v(), cm[1].v(), pts[mt].v(), start=(mt == 0), stop=(mt == nmt - 1))
                    rinv = ftmp()
                    P.act(rinv.v(), accb[2].v(), AF.Ln)
                    P.act(rinv.v(), rinv.v(), AF.Exp, scale=-1.0)
                    for dvt in range(2):
                        pz, _ = proj_fm("mmz", h * 256 + dvt * 128, hT)
                        sz = ftmp()
                        P.act(sz.v(), pz.v(), AF.Silu)
                        t_ = ftmp()
                        P.tt("dve", t_.v(), accb[dvt].v(), rinv.v(), ALU.mult)
                        P.tt("dve", actb[h * 2 + dvt].v(), t_.v(), sz.v(), ALU.mult)
                out_branch("w_mem_out", actb[0:KM], "gmem", False)

                cqf = []
                for blk in range(KQ):
                    pq, _ = proj_fm("cq", blk * 128, hT)
                    fb = ftmp()
                    P.copy("act", fb.v(), pq.v())
                    cqf.append(fb)
                rs = ftmp()
                rms_feat(P, ps, [q.v() for q in cqf], SEG, cm[QL], rs, sqb=sqb)
                for blk in range(KQ):
                    P.stt(cqn[blk].v(), cqf[blk].v(), G("qn", blk), rs.v(), ALU.mult, ALU.mult)
                nkc = 2 * m + 2
                nkt = nkc * 8
                LOOK = 2

                def q_prologue(h):
                    qn_b, qr_b = Qn[h % 2], Qr[h % 2]
                    wq_ = wcols("w_uq", h * 192, need=192)
                    wqs_ = wcols("w_uqsw", h * 64, need=64)
                    pb = ps()
                    for kt in range(KQ):
                        P.mm(pb.v(), wq_(kt, 0, 128), cqn[kt].v(), start=(kt == 0), stop=(kt == KQ - 1))
                    qf = ftmp()
                    P.copy("act", qf.v(), pb.v())
                    rs = ftmp()
                    rms_feat(P, ps, [qf.v()], SEG, cm[128], rs, sqb=sqb)
                    P.stt(qn_b.v(), qf.v(), G("qnn"), rs.v(), ALU.mult, ALU.mult)
                    pr = ps()
                    for kt in range(KQ):
                        P.mm(pr[0:64, :], wq_(kt, 128, 192), cqn[kt].v(), start=(kt == 0), stop=(kt == KQ - 1))
                    prs = ps()
                    for kt in range(KQ):
                        P.mm(prs[0:64, :], wqs_(kt, 0, 64), cqn[kt].v(), start=(kt == 0), stop=(kt == KQ - 1))
                    rs = ftmp()
                    rms_feat(P, ps, [pr[0:64, :]], SEG, cm[64], rs, rows=64, sqb=sqb)
                    P.stt(tm1.v(), pr[0:64, :], G("qnr", 0, 64), cosT.v(), ALU.mult, ALU.mult)
                    P.stt(tm2.v(), prs[0:64, :], G("qnrsw", 0, 64), sinT.v(), ALU.mult, ALU.mult)
                    P.tt("dve", tm1.v(), tm1.v(), tm2.v(), ALU.add)
                    P.tt("dve", qr_b.v(), tm1.v(), rs[0:64, :], ALU.mult)

                def attend(h):
                    qn_b, qr_b = Qn[h % 2], Qr[h % 2]
                    pO, pS = accb[2 * (h % 2)], accb[2 * (h % 2) + 1]
                    bufs_ = {}
                    for i in range(nkt + LOOK):
                        if i < nkt:
                            kc, k8 = i // 8, i % 8
                            if k8 == 0:
                                ci = kvstate["i"] % len(Kc)
                                kvstate["i"] += 1
                                bufs_[kc] = (Kc[ci], Vc[ci])
                                P.dma("sp", Kc[ci].v(), View(None, K_all[:, h, kc * 1024:(kc + 1) * 1024]))
                                P.dma("sp", Vc[ci].v(), View(None, V_all[:, h, kc * 8:(kc + 1) * 8, :]))
                            kcb = bufs_[kc][0]
                            pb = ps()
                            P.mm(pb.v(), kcb[:, k8 * 128:(k8 + 1) * 128], qn_b.v(), start=True, stop=False)
                            P.mm(pb.v(), kropeT[:, i * 128:(i + 1) * 128], qr_b.v(), start=False, stop=True)
                            pt = Pt[i % 4]
                            P.act(pt.v(), pb.v(), AF.Exp, scale=scale_mla)
                            if i >= 16 * m:
                                r_ = (i - 16 * m) // 4
                                ki_ = (i - 16 * m) % 4
                                P.ts("dve", tmpm.v(), diagm[:, ki_ * SEG:(ki_ + 1) * SEG], abt[:, 2 * r_:2 * r_ + 1],
                                     abt[:, 2 * r_ + 1:2 * r_ + 2], ALU.mult, ALU.add)
                                P.tt("dve", pt.v(), pt.v(), tmpm.v(), ALU.mult)
                        j = i - LOOK
                        if j >= 0:
                            vcb = bufs_[j // 8][1]
                            pt = Pt[j % 4]
                            P.mm(pO.v(), vcb[:, j % 8, :], pt.v(), start=(j == 0), stop=(j == nkt - 1))
                            P.mm(pS.v(), cm[1].v(), pt.v(), start=(j == 0), stop=(j == nkt - 1))

                def epilogue(h):
                    pO, pS = accb[2 * (h % 2)], accb[2 * (h % 2) + 1]
                    rinv = ftmp()
                    P.act(rinv.v(), pS.v(), AF.Ln)
                    P.act(rinv.v(), rinv.v(), AF.Exp, scale=-1.0)
                    pz, _ = proj_fm("mz", h * 128, hT)
                    sz = ftmp()
                    P.act(sz.v(), pz.v(), AF.Silu)
                    t_ = ftmp()
                    P.tt("dve", t_.v(), pO.v(), rinv.v(), ALU.mult)
                    P.tt("dve", mlab[h].v(), t_.v(), sz.v(), ALU.mult)

                q_prologue(0)
                for h in range(H):
                    if h + 1 < H:
                        q_prologue(h + 1)
                    attend(h)
                    epilogue(h)
                out_branch("w_mla_out", mlab, "gm", False)

                for k in range(KD):
                    P.copy("act" if k % 2 == 0 else "dve", hT[k].v(), merged[k].v())
                NCO = streams["w_o_a"]["ncol"]
                NCB = D // NCO
                stg = {}
                ei = 0
                for cb in range(NCB):
                    ga = wcols("w_o_a", cb * NCO, need=NCO)
                    gb = wcols("w_o_b", cb * NCO, need=NCO) if KD > 1 else None
                    for tt_ in range(4):
                        pb = ps()
                        for kt in range(KD):
                            rhs = ga(kt) if kt < KDH else gb(kt - KDH)
                            P.mm(pb[:, 0:NCO], hT[kt][:, tt_ * 128:(tt_ + 1) * 128], rhs, start=(kt == 0),
                                 stop=(kt == KD - 1))
                        q_ = (tt_ * NCB + cb) * NCO
                        mb_, mo_ = merged[q_ // SEG], q_ % SEG
                        stg[(tt_, cb)] = (mb_, mo_)
                        P.copy("act" if ei % 2 == 0 else "dve", mb_[:, mo_:mo_ + NCO], pb[:, 0:NCO])
                        ei += 1
                for tt_ in range(4):
                    r0 = m * SEG + tt_ * 128
                    xr = xts[tt_ % 2]
                    P.dma("sp", xr.v(), X(xown[r0:r0 + 128, :]))
                    for cb in range(NCB):
                        mb_, mo_ = stg[(tt_, cb)]
                        P.tt("dve", xr[:, cb * NCO:(cb + 1) * NCO], mb_[:, mo_:mo_ + NCO],
                             xr[:, cb * NCO:(cb + 1) * NCO], ALU.add)
                    P.dma("sp", View(None, outd[r0:r0 + 128, :]), xr.v())
            P.wait_all_dma("sp")
            P.emit()
    return nc


def host_inputs(cfg, inp):
    d = dims(cfg)
    D, SEQ, B, CW, H, QL, KVL, MH, MEMT = (cfg[k] for k in ("D", "SEQ", "B", "CW", "H", "QL", "KVL", "MH", "MEMT"))
    KD, NST, NSEG, MW, NG = d["KD"], d["NST"], d["NSEG"], d["MW"], d["NG"]
    offs, gcol = d["offs"], d["gcol"]
    f = lambda a: np.ascontiguousarray(np.asarray(a, dtype=np.float32))
    x = f(inp["x"])
    pos = np.ascontiguousarray(np.asarray(inp["positions"], dtype=np.int32))
    mem = f(inp["mem"])
    w_in = f(inp["w_in"][0])
    KC = CW // 128

    def cols(v, n):
        return np.asarray(v, np.float32).reshape(n, 128).T

    gv = np.zeros((128, NG), np.float32)
    gv[:, gcol["norm"]:gcol["norm"] + KD] = cols(inp["norm_g"][0], KD)
    gv[:, gcol["memnorm"]:gcol["memnorm"] + KD] = cols(inp["mem_norm_g"][0], KD)
    gv[:, gcol["qn"]:gcol["qn"] + QL // 128] = cols(inp["mla_q_norm_g"][0], QL // 128)
    gv[:, gcol["kvn"]:gcol["kvn"] + KVL // 128] = cols(inp["mla_kv_norm_g"][0], KVL // 128)
    gv[:, gcol["qnn"]] = np.asarray(inp["mla_qn_nope_g"][0], np.float32)
    gv[:, gcol["knn"]] = np.asarray(inp["mla_kn_nope_g"][0], np.float32)
    qr = np.asarray(inp["mla_qn_rope_g"][0], np.float32)
    kr = np.asarray(inp["mla_kn_rope_g"][0], np.float32)
    sw = lambda v: np.concatenate([v[32:64], v[0:32]])
    gv[0:64, gcol["qnr"]] = qr
    gv[0:64, gcol["qnrsw"]] = sw(qr)
    gv[0:64, gcol["knr"]] = kr
    gv[0:64, gcol["knrsw"]] = sw(kr)
    gv[:, gcol["memq"]:gcol["memq"] + 2] = cols(inp["mem_qn_g"][0], 2)
    gv[:, gcol["memk"]:gcol["memk"] + 2] = cols(inp["mem_kn_g"][0], 2)
    cw = np.asarray(inp["conv_w"][0], np.float32)
    for j in range(3):
        gv[:, gcol["conv"] + j * KC:gcol["conv"] + (j + 1) * KC] = cols(cw[j], KC)
    invf = np.power(np.float32(10000.0), -np.arange(32, dtype=np.float32) / np.float32(32)).astype(np.float32)
    gv[0:64, gcol["invf"]] = np.concatenate([invf, invf])
    gv[0:64, gcol["sgn"]] = np.concatenate([-np.ones(32, np.float32), np.ones(32, np.float32)])

    blocks = []
    for j in range(KC):
        for nm in ("cg", "bg", "u", "cz"):
            blocks.append(w_in[:, offs[nm] + j * 128:offs[nm] + (j + 1) * 128])
    w_conv4 = np.ascontiguousarray(np.concatenate(blocks, axis=1))
    a = offs["kr"]
    w_krsw = np.ascontiguousarray(np.concatenate([w_in[:, a + 32:a + 64], w_in[:, a:a + 32]], axis=1))
    w_uq = f(inp["w_uq"][0])
    w_uqsw = np.ascontiguousarray(np.concatenate(
        [np.concatenate([w_uq[:, h * 192 + 160:h * 192 + 192], w_uq[:, h * 192 + 128:h * 192 + 160]], axis=1)
         for h in range(H)], axis=1))
    w_ukv = f(inp["w_ukv"][0])
    w_ukv_k = np.ascontiguousarray(np.concatenate([w_ukv[:, h * 256:h * 256 + 128] for h in range(H)], axis=1))
    w_ukv_v = np.ascontiguousarray(np.concatenate([w_ukv[:, h * 256 + 128:h * 256 + 256] for h in range(H)], axis=1))
    ident = np.eye(128, dtype=np.float32)
    kk = np.arange(SEG)[:, None] // 64
    qq = np.arange(SEG)[None, :] // 64
    diag = (kk <= qq).astype(np.float32)
    diagm = np.ascontiguousarray(diag.reshape(4, 128, SEG).transpose(1, 0, 2).reshape(128, 4 * SEG))
    common = dict(ident=ident, diagm=diagm, gv=gv, w_in=w_in, w_conv4=w_conv4, w_krsw=w_krsw, w_uq=w_uq, w_uqsw=w_uqsw,
                  w_ukv_k=w_ukv_k, w_ukv_v=w_ukv_v, w_conv_out=f(inp["w_conv_out"][0]),
                  w_mla_out=f(inp["w_mla_out"][0]), w_mem_kv=f(inp["w_mem_kv"][0]),
                  w_mem_out=f(inp["w_mem_out"][0]), w_o=f(inp["w_o"][0]))
    maps = []
    for core in range(4 * B):
        b, c = core // 4, core % 4
        segs = [4 * m + c for m in range(NSEG)]
        xown = np.concatenate([x[b, s * SEG:(s + 1) * SEG] for s in segs], axis=0)
        xhalo = np.zeros((2 * NSEG, D), np.float32)
        for m, s in enumerate(segs):
            if s > 0:
                xhalo[2 * m:2 * m + 2] = x[b, s * SEG - 2:s * SEG]
        posown = np.concatenate([pos[b, s * SEG:(s + 1) * SEG] for s in segs])[None, :]
        ab = np.zeros((128, 8), np.float32)
        for r in range(4):
            if r < c:
                ab[:, 2 * r + 1] = 1.0
            elif r == c:
                ab[:, 2 * r] = 1.0
        mp = dict(common)
        mp.update(xall=np.ascontiguousarray(x[b]), xown=np.ascontiguousarray(xown), xhalo=xhalo,
                  posall=np.ascontiguousarray(pos[b][None, :]), posown=np.ascontiguousarray(posown),
                  memb=np.ascontiguousarray(mem[b]), ab=ab)
        maps.append(mp)
    return maps


def assemble(cfg, results):
    d = dims(cfg)
    D, SEQ, B = cfg["D"], cfg["SEQ"], cfg["B"]
    NSEG = d["NSEG"]
    out = np.zeros((B, SEQ, D), np.float32)
    for core in range(4 * B):
        b, c = core // 4, core % 4
        o = results[core]["out"]
        for m in range(NSEG):
            s = 4 * m + c
            out[b, s * SEG:(s + 1) * SEG] = o[m * SEG:(m + 1) * SEG]
    return out


def kernel(**inputs):
    cfg = CFG
    nc = build_nc(cfg)
    maps = host_inputs(cfg, inputs)
    res = run_bass_kernel_spmd(nc, maps, core_ids=list(range(4 * cfg["B"])))
    return assemble(cfg, res.results)
```
